# Optimizing a Trainium2 kernel written in Bass

```python
import math
import jax
import jax.numpy as jnp
from jax import lax
import numpy as np

D_MODEL = 4096
BATCH = 2
SEQ = 8192
DEPTH = 2

CTX_LEN = 256
GRID_W = 64
NORM_EPS = 1e-6
N_MOD = 6

RW_WIDTH = D_MODEL // 4
RW_HEAD = 64
RW_HEADS = RW_WIDTH // RW_HEAD
RW_DECAY_LORA = max(32, int(round(D_MODEL ** 0.5 * 1.8 / 32)) * 32)
RW_AAA_LORA = max(32, int(round(D_MODEL ** 0.5 * 1.8 / 32)) * 32)
RW_GATE_LORA = max(32, int(round(D_MODEL ** 0.6 * 0.6 / 32)) * 32)
RW_GN_EPS = 64e-5
RW_COLS = 3 * RW_WIDTH + 2 * RW_DECAY_LORA + 2 * RW_AAA_LORA + RW_GATE_LORA
RW_SPLIT = (RW_WIDTH, 2 * RW_WIDTH, 3 * RW_WIDTH,
            3 * RW_WIDTH + RW_DECAY_LORA, 3 * RW_WIDTH + 2 * RW_DECAY_LORA,
            3 * RW_WIDTH + 2 * RW_DECAY_LORA + RW_AAA_LORA, 3 * RW_WIDTH + 2 * RW_DECAY_LORA + 2 * RW_AAA_LORA)

HG_WIDTH = D_MODEL // 4
HG_EXPAND = 128
HG_HEADS = HG_WIDTH // HG_EXPAND
HG_VDIM = HG_WIDTH // HG_HEADS
HG_CHUNK = 64
HG_COLS = 5 * HG_WIDTH

HY_WIDTH = D_MODEL // 4
HY_EMB = 33
HY_BANDS = (HY_EMB - 1) // 2
HY_FILTER_HIDDEN = 64
HY_SHORT = 3
HY_DECAY_TARGET = 1e-2
HY_FAST_PCT = 0.3
HY_SLOW_PCT = 1.5
HY_COLS = 3 * HY_WIDTH

N_BRANCH = 3
GATE_COLS = N_BRANCH * D_MODEL
HG_END = RW_COLS + HG_COLS
HY_END = HG_END + HY_COLS
IN_COLS = HY_END + GATE_COLS
FFN_HIDDEN = -(-(8 * D_MODEL) // (3 * 256)) * 256

kernel_name = 'hybrid_rwkv7_hgrn2_hyena_prefix_dit'


def _rmsnorm(x, gain):
    xf = x.astype(jnp.float32)
    y = xf * lax.rsqrt(jnp.mean(xf * xf, axis=-1, keepdims=True) + NORM_EPS)
    return (y * gain.astype(jnp.float32)).astype(x.dtype)


def _modulate(x, gain, shift, scale):
    return _rmsnorm(x, gain) * (1.0 + scale) + shift


def _heads(t, n_heads):
    return t.reshape(t.shape[:-1] + (n_heads, t.shape[-1] // n_heads))


def _grid_quad_shift(p):
    b, n, ch = p.shape
    rows = n // GRID_W
    g = p.reshape(b, rows, GRID_W, ch)
    q = ch // 4
    left = jnp.pad(g[:, :, :-1, :q], ((0, 0), (0, 0), (1, 0), (0, 0)))
    right = jnp.pad(g[:, :, 1:, q:2 * q], ((0, 0), (0, 0), (0, 1), (0, 0)))
    up = jnp.pad(g[:, :-1, :, 2 * q:3 * q], ((0, 0), (1, 0), (0, 0), (0, 0)))
    down = jnp.pad(g[:, 1:, :, 3 * q:], ((0, 0), (0, 1), (0, 0), (0, 0)))
    return jnp.concatenate([left, right, up, down], axis=-1).reshape(b, n, ch)


def _seq_bi_shift(p):
    h = p.shape[-1] // 2
    left = jnp.pad(p[:, :-1, :h], ((0, 0), (1, 0), (0, 0)))
    right = jnp.pad(p[:, 1:, h:], ((0, 0), (0, 1), (0, 0)))
    return jnp.concatenate([left, right], axis=-1)


def _rwkv7_scan(r, w, k, v, a, b, s0):
    def step(s, inp):
        rt, wt, kt, vt, at, bt = inp
        sa = jnp.einsum('bhvk,bhk->bhv', s, at)
        s = s * wt[:, :, None, :] + sa[..., None] * bt[:, :, None, :] + vt[..., None] * kt[:, :, None, :]
        return s, jnp.einsum('bhvk,bhk->bhv', s, rt)
    xs = tuple(jnp.moveaxis(t.astype(jnp.float32), 1, 0) for t in (r, w, k, v, a, b))
    s_fin, ys = lax.scan(step, s0, xs)
    return jnp.moveaxis(ys, 0, 1), s_fin


def _rwkv7_prepare(p, shifted, mu, w0, w_up, a0, a_up, g_up, k_k, k_a):
    m = (p + mu * (shifted - p)).astype(jnp.float32)
    r, k, v, wd_f, wd_b, ad_f, ad_b, gd = jnp.split(m, RW_SPLIT, axis=-1)
    g = jax.nn.sigmoid(gd) @ g_up
    kk = _heads(k * k_k, RW_HEADS)
    kk = kk * lax.rsqrt(jnp.maximum(jnp.sum(kk * kk, axis=-1, keepdims=True), 1e-24))
    per_dir = []
    for d, (wd, ad) in enumerate(((wd_f, ad_f), (wd_b, ad_b))):
        w_log = -jax.nn.softplus(-(w0[d] + jnp.tanh(wd) @ w_up[d])) - 0.5
        decay = jnp.exp(-jnp.exp(w_log))
        a = jax.nn.sigmoid(a0[d] + ad @ a_up[d])
        kd = k * (1.0 + (a - 1.0) * k_a)
        per_dir.append((_heads(decay, RW_HEADS), _heads(kd, RW_HEADS), _heads(a, RW_HEADS)))
    return _heads(r, RW_HEADS), _heads(v, RW_HEADS), kk, g, per_dir


def _group_norm(y, gain, bias):
    mean = jnp.mean(y, axis=-1, keepdims=True)
    var = jnp.mean(jnp.square(y - mean), axis=-1, keepdims=True)
    yn = (y - mean) * lax.rsqrt(var + RW_GN_EPS)
    b, n = y.shape[:2]
    return yn.reshape(b, n, -1) * gain + bias


def _rwkv7_mixer(p_c, p_l, params, need_ctx):
    mu, w0, w_up, a0, a_up, g_up, k_k, k_a, r_k, ln_g, ln_b = params
    lora = (mu, w0, w_up, a0, a_up, g_up, k_k, k_a)
    prep_c = _rwkv7_prepare(p_c, _seq_bi_shift(p_c), *lora)
    prep_l = _rwkv7_prepare(p_l, _grid_quad_shift(p_l), *lora)
    s0 = jnp.zeros((p_l.shape[0], RW_HEADS, RW_HEAD, RW_HEAD), jnp.float32)

    def run(prep, d, s_init):
        r, v, kk, _, per_dir = prep
        decay, kd, a = per_dir[d]
        seqs = (r, decay, kd, v, -kk, kk * a)
        if d == 1:
            seqs = tuple(t[:, ::-1] for t in seqs)
        y, s = _rwkv7_scan(*seqs, s_init)
        return (y[:, ::-1] if d == 1 else y), s

    def finish(prep, y):
        r, v, _, g, per_dir = prep
        bonus = sum(jnp.sum(r * pd[1] * r_k, axis=-1, keepdims=True) for pd in per_dir) * v
        b, n = y.shape[:2]
        return (_group_norm(y, ln_g, ln_b) + bonus.reshape(b, n, RW_WIDTH)) * g

    yc_f, sc_f = run(prep_c, 0, s0)
    yc_b, sc_b = run(prep_c, 1, s0)
    yl_f, _ = run(prep_l, 0, sc_f)
    yl_b, _ = run(prep_l, 1, sc_b)
    out_l = finish(prep_l, yl_f + yl_b)
    out_c = finish(prep_c, yc_f + yc_b) if need_ctx else None
    return out_c, out_l


def _hgrn2_lower_bounds(lower_bounds, layer):
    cum = jnp.cumsum(jax.nn.softmax(lower_bounds.astype(jnp.float32), axis=0), axis=0)
    return cum[layer] - cum[0]


def _hgrn2_chunk_scan(q, k, v, log_f, s0):
    b, n, h, _ = q.shape
    nc = n // HG_CHUNK

    def to_chunks(t):
        return t.reshape(b, nc, HG_CHUNK, h, t.shape[-1]).transpose(1, 0, 3, 2, 4)

    mask = jnp.tril(jnp.ones((HG_CHUNK, HG_CHUNK), dtype=bool))[:, :, None]

    def step(s, inp):
        qt, kt, vt, gt = inp
        cb = jnp.cumsum(gt, axis=2)
        rel = cb[:, :, :, None, :] - cb[:, :, None, :, :]
        dec = jnp.exp(jnp.where(mask, rel, -jnp.inf))
        att = jnp.einsum('bhtk,bhtsk,bhsk->bhts', qt, dec, kt)
        o = jnp.einsum('bhts,bhsv->bhtv', att, vt) + jnp.einsum('bhtk,bhkv->bhtv', qt * jnp.exp(cb), s)
        last = cb[:, :, -1:, :]
        s_new = jnp.exp(last[:, :, 0, :])[..., None] * s + jnp.einsum('bhsk,bhsv->bhkv', kt * jnp.exp(last - cb), vt)
        return s_new, o

    xs = tuple(to_chunks(t.astype(jnp.float32)) for t in (q, k, v, log_f))
    s_fin, oc = lax.scan(step, s0, xs)
    return oc.transpose(1, 0, 3, 2, 4).reshape(b, n, h, -1), s_fin


def _hgrn2_prepare(p, lb):
    q, z_f, z_b, i, g = jnp.split(p.astype(jnp.float32), 5, axis=-1)
    gates = []
    for d, z in enumerate((z_f, z_b)):
        z = _heads(z, HG_HEADS)
        l = lb[d].reshape(HG_HEADS, HG_EXPAND)
        log_f = jnp.logaddexp(jnp.log(l), jnp.log1p(-l) + jax.nn.log_sigmoid(z))
        k = (1.0 - l) * jax.nn.sigmoid(-z)
        gates.append((log_f, k))
    return _heads(q, HG_HEADS), _heads(i, HG_HEADS), g, gates


def _hgrn2_mixer(p_c, p_l, lb, norm_g, need_ctx):
    prep_c = _hgrn2_prepare(p_c, lb)
    prep_l = _hgrn2_prepare(p_l, lb)
    s0 = jnp.zeros((p_l.shape[0], HG_HEADS, HG_EXPAND, HG_VDIM), jnp.float32)

    def run(prep, d, s_init):
        q, i, _, gates = prep
        log_f, k = gates[d]
        seqs = (q, k, i, log_f)
        if d == 1:
            seqs = tuple(t[:, ::-1] for t in seqs)
        o, s = _hgrn2_chunk_scan(*seqs, s_init)
        return (o[:, ::-1] if d == 1 else o), s

    def finish(prep, o):
        b, n = o.shape[:2]
        return _rmsnorm(o, norm_g).reshape(b, n, HG_WIDTH) * jax.nn.silu(prep[2])

    oc_f, sc_f = run(prep_c, 0, s0)
    oc_b, sc_b = run(prep_c, 1, s0)
    ol_f, _ = run(prep_l, 0, sc_f)
    ol_b, _ = run(prep_l, 1, sc_b)
    out_l = finish(prep_l, ol_f + ol_b)
    out_c = finish(prep_c, oc_f + oc_b) if need_ctx else None
    return out_c, out_l


def _short_conv(p, w, bias):
    pp = jnp.pad(p, ((0, 0), (1, 1), (0, 0)))
    return w[0] * pp[:, :-2] + w[1] * pp[:, 1:-1] + w[2] * pp[:, 2:] + bias


def _hyena_filters(n, w1, b1, w2, b2, w3, freq):
    t = jnp.linspace(0.0, 1.0, n, dtype=jnp.float32)[:, None]
    lag = jnp.arange(n, dtype=jnp.float32)[:, None]
    bands = jnp.linspace(1e-4, HY_BANDS - 1, HY_BANDS, dtype=jnp.float32)[None, :]
    ang = 2.0 * math.pi * lag * bands / n
    z = jnp.concatenate([t, jnp.cos(ang), -jnp.sin(ang)], axis=-1)
    h = jnp.sin(freq * (z @ w1 + b1))
    h = jnp.sin(freq * (h @ w2 + b2))
    h = (h @ w3).astype(jnp.float32)
    deltas = jnp.abs(jnp.linspace(math.log(HY_DECAY_TARGET) / HY_SLOW_PCT,
                                  math.log(HY_DECAY_TARGET) / HY_FAST_PCT, HY_WIDTH, dtype=jnp.float32))
    window = jnp.exp(-t * deltas[None, :])
    h_f, h_b = jnp.split(h, 2, axis=-1)
    return h_f * window, h_b * window


def _bidir_long_conv(u, h_f, h_b):
    n = u.shape[1]
    taps = jnp.concatenate([0.5 * (h_f[:1] + h_b[:1]), h_f[1:], jnp.zeros_like(h_f[:1]), h_b[1:][::-1]], axis=0)
    taps = taps / jnp.sum(jnp.abs(taps), axis=0, keepdims=True)
    u_f = jnp.fft.rfft(u.astype(jnp.float32), n=2 * n, axis=1)
    t_f = jnp.fft.rfft(taps, n=2 * n, axis=0)
    return jnp.fft.irfft(u_f * t_f[None], n=2 * n, axis=1)[:, :n]


def _hyena_branch(p, conv_w, conv_b, filt, bias):
    u = _short_conv(p, conv_w, conv_b)
    x0, x1, v = jnp.split(u, 3, axis=-1)
    h_f, h_b = _hyena_filters(p.shape[1], *filt)
    z = (x1 * v).astype(jnp.float32)
    z = _bidir_long_conv(z, h_f, h_b) + bias * z
    return x0 * z


def _merge(p_gate, oa, ob, oc, w_ba, w_bb, w_bc, w_out):
    ga, gb, gc = jnp.split(jax.nn.sigmoid(p_gate), N_BRANCH, axis=-1)
    y = ga * (oa @ w_ba) + gb * (ob @ w_bb) + gc * (oc @ w_bc)
    return y @ w_out


def _swiglu(h, w_gate, w_up, w_down):
    return (jax.nn.silu(h @ w_gate) * (h @ w_up)) @ w_down


def _layer(xl, xc, c, c_ctx, ada_w, ada_b, n1, n2, w_in, rw, hg, hy, merge, ffn, need_ctx):
    mod_l = jnp.split((jax.nn.silu(c) @ ada_w + ada_b)[:, None, :], N_MOD, axis=-1)
    mod_c = jnp.split(jax.nn.silu(c_ctx) @ ada_w + ada_b, N_MOD, axis=-1)
    p_l = _modulate(xl, n1, mod_l[0], mod_l[1]) @ w_in
    w_in_c = w_in if need_ctx else w_in[:, :HG_END]
    p_c = _modulate(xc, n1, mod_c[0], mod_c[1]) @ w_in_c
    oa_c, oa_l = _rwkv7_mixer(p_c[..., :RW_COLS], p_l[..., :RW_COLS], rw, need_ctx)
    ob_c, ob_l = _hgrn2_mixer(p_c[..., RW_COLS:HG_END], p_l[..., RW_COLS:HG_END], hg[0], hg[1], need_ctx)
    oc_l = _hyena_branch(p_l[..., HG_END:HY_END], *hy)
    xl = xl + mod_l[2] * _merge(p_l[..., HY_END:], oa_l, ob_l, oc_l, *merge)
    xl = xl + mod_l[5] * _swiglu(_modulate(xl, n2, mod_l[3], mod_l[4]), *ffn)
    if not need_ctx:
        return xl, None
    oc_c = _hyena_branch(p_c[..., HG_END:HY_END], *hy)
    xc = xc + mod_c[2] * _merge(p_c[..., HY_END:], oa_c, ob_c, oc_c, *merge)
    xc = xc + mod_c[5] * _swiglu(_modulate(xc, n2, mod_c[3], mod_c[4]), *ffn)
    return xl, xc


def setup_inputs(seed: int = 0) -> dict:
    key = jax.random.key(seed)
    ks = iter(jax.random.split(key, 48))
    f32 = jnp.float32

    def nrm(shape, scale):
        return jax.random.normal(next(ks), shape, f32) * scale

    L = DEPTH
    w0_base = -6.0 + 5.0 * (jnp.arange(RW_WIDTH, dtype=f32) / (RW_WIDTH - 1)) ** 0.85 + 0.5
    return {
        'x': nrm((BATCH, SEQ, D_MODEL), 1.0),
        'c': nrm((BATCH, D_MODEL), 1.0),
        'ctx': nrm((BATCH, CTX_LEN, D_MODEL), 1.0),
        'c_ctx': nrm((D_MODEL,), 1.0),
        'ada_w': nrm((L, D_MODEL, N_MOD * D_MODEL), 0.5 * D_MODEL ** -0.5),
        'ada_b': nrm((L, N_MOD * D_MODEL), 0.01),
        'norm1_g': 1.0 + nrm((L, D_MODEL), 0.02),
        'norm2_g': 1.0 + nrm((L, D_MODEL), 0.02),
        'w_in': nrm((L, D_MODEL, IN_COLS), D_MODEL ** -0.5),
        'rw_mu': jax.random.uniform(next(ks), (L, RW_COLS), f32),
        'rw_w0': w0_base + nrm((L, 2, RW_WIDTH), 0.1),
        'rw_w_up': nrm((L, 2, RW_DECAY_LORA, RW_WIDTH), 0.1 * RW_DECAY_LORA ** -0.5),
        'rw_a0': nrm((L, 2, RW_WIDTH), 0.1),
        'rw_a_up': nrm((L, 2, RW_AAA_LORA, RW_WIDTH), 0.3 * RW_AAA_LORA ** -0.5),
        'rw_g_up': nrm((L, RW_GATE_LORA, RW_WIDTH), RW_GATE_LORA ** -0.5),
        'rw_k_k': 0.85 + nrm((L, RW_WIDTH), 0.02),
        'rw_k_a': 1.0 + nrm((L, RW_WIDTH), 0.02),
        'rw_r_k': -0.04 + nrm((L, RW_HEADS, RW_HEAD), 0.02),
        'rw_ln_g': 1.0 + nrm((L, RW_WIDTH), 0.02),
        'rw_ln_b': nrm((L, RW_WIDTH), 0.01),
        'hg_lower_bounds': nrm((L, 2, HG_WIDTH), 1.0),
        'hg_norm_g': 1.0 + nrm((L, HG_VDIM), 0.02),
        'hy_conv_w': nrm((L, HY_SHORT, HY_COLS), HY_SHORT ** -0.5),
        'hy_conv_b': nrm((L, HY_COLS), 0.01),
        'hy_f_w1': nrm((L, HY_EMB, HY_FILTER_HIDDEN), HY_EMB ** -0.5),
        'hy_f_b1': nrm((L, HY_FILTER_HIDDEN), 0.1),
        'hy_f_w2': nrm((L, HY_FILTER_HIDDEN, HY_FILTER_HIDDEN), HY_FILTER_HIDDEN ** -0.5),
        'hy_f_b2': nrm((L, HY_FILTER_HIDDEN), 0.1),
        'hy_f_w3': nrm((L, HY_FILTER_HIDDEN, 2 * HY_WIDTH), HY_FILTER_HIDDEN ** -0.5),
        'hy_freq': 1.0 + nrm((L, HY_FILTER_HIDDEN), 0.1),
        'hy_bias': nrm((L, HY_WIDTH), 1.0),
        'w_branch_a': nrm((L, RW_WIDTH, D_MODEL), RW_WIDTH ** -0.5),
        'w_branch_b': nrm((L, HG_WIDTH, D_MODEL), HG_WIDTH ** -0.5),
        'w_branch_c': nrm((L, HY_WIDTH, D_MODEL), HY_WIDTH ** -0.5),
        'w_out': nrm((L, D_MODEL, D_MODEL), D_MODEL ** -0.5),
        'ffn_w_gate': nrm((L, D_MODEL, FFN_HIDDEN), D_MODEL ** -0.5),
        'ffn_w_up': nrm((L, D_MODEL, FFN_HIDDEN), D_MODEL ** -0.5),
        'ffn_w_down': nrm((L, FFN_HIDDEN, D_MODEL), FFN_HIDDEN ** -0.5),
        'final_norm_g': 1.0 + nrm((D_MODEL,), 0.02),
    }


def reference(x, c, ctx, c_ctx, ada_w, ada_b, norm1_g, norm2_g, w_in,
              rw_mu, rw_w0, rw_w_up, rw_a0, rw_a_up, rw_g_up, rw_k_k, rw_k_a, rw_r_k, rw_ln_g, rw_ln_b,
              hg_lower_bounds, hg_norm_g,
              hy_conv_w, hy_conv_b, hy_f_w1, hy_f_b1, hy_f_w2, hy_f_b2, hy_f_w3, hy_freq, hy_bias,
              w_branch_a, w_branch_b, w_branch_c, w_out,
              ffn_w_gate, ffn_w_up, ffn_w_down, final_norm_g):
    xl, xc = x, ctx
    for layer in range(DEPTH):
        need_ctx = layer < DEPTH - 1
        rw = (rw_mu[layer], rw_w0[layer], rw_w_up[layer], rw_a0[layer], rw_a_up[layer], rw_g_up[layer],
              rw_k_k[layer], rw_k_a[layer], rw_r_k[layer], rw_ln_g[layer], rw_ln_b[layer])
        hg = (_hgrn2_lower_bounds(hg_lower_bounds, layer), hg_norm_g[layer])
        hy = (hy_conv_w[layer], hy_conv_b[layer],
              (hy_f_w1[layer], hy_f_b1[layer], hy_f_w2[layer], hy_f_b2[layer], hy_f_w3[layer], hy_freq[layer]),
              hy_bias[layer])
        merge = (w_branch_a[layer], w_branch_b[layer], w_branch_c[layer], w_out[layer])
        ffn = (ffn_w_gate[layer], ffn_w_up[layer], ffn_w_down[layer])
        xl, xc = _layer(xl, xc, c, c_ctx, ada_w[layer], ada_b[layer], norm1_g[layer], norm2_g[layer],
                        w_in[layer], rw, hg, hy, merge, ffn, need_ctx)
    return _rmsnorm(xl, final_norm_g)
```

```python
import contextlib
import numpy as np
import concourse.bass as bass
import concourse.mybir as mybir
from concourse.bass_utils import run_bass_kernel_spmd

F32 = mybir.dt.float32
AF = mybir.ActivationFunctionType
ALU = mybir.AluOpType
AX = mybir.AxisListType


class KB:
    EPOCH = 20000
    NSLOT = 6
    EMBED = True

    def __init__(self, nc):
        self.nc = nc
        self.es = contextlib.ExitStack()
        self.eng = {'pe': nc.tensor, 'act': nc.scalar, 'dve': nc.vector,
                    'pool': nc.gpsimd, 'sp': nc.sync}
        self.csem = {}
        self.ccnt = {}
        self.seen = {}
        self.last_w = {}
        self.readers = {}
        self.slots = {}
        self.slot_rr = {}
        self.nsem = 0
        self.n_inst = 0
        self.uid = 0

    def new_sem(self, name):
        self.nsem += 1
        return self.es.enter_context(self.nc.semaphore(f"{name}_{self.nsem}"))

    def sb(self, name, shape, dtype=F32):
        self.uid += 1
        return self.es.enter_context(self.nc.sbuf_tensor(f"{name}_{self.uid}", list(shape), dtype))

    def ps(self, name, shape, dtype=F32):
        self.uid += 1
        return self.es.enter_context(self.nc.psum_tensor(f"{name}_{self.uid}", list(shape), dtype))

    def dram(self, name, shape, dtype=F32, kind="Internal"):
        return self.nc.dram_tensor(name, list(shape), dtype, kind=kind)

    def _need(self, reads, writes):
        need = []
        for k in reads:
            t = self.last_w.get(k)
            if t is not None:
                need.append(t)
        for k in writes:
            t = self.last_w.get(k)
            if t is not None:
                need.append(t)
            need.extend(self.readers.get(k, ()))
        return need

    def _emit_waits(self, e, need, is_pe_mm=False, embed=False):
        best = {}
        for (sem, val, pe) in need:
            if is_pe_mm and pe == 'pe':
                continue
            sid = id(sem)
            if self.seen.get((e, sid), 0) >= val:
                continue
            if sid not in best or best[sid][1] < val:
                best[sid] = (sem, val)
        items = list(best.items())
        held = None
        if embed and items:
            held = items.pop()
        for sid, (sem, val) in items:
            self.eng[e].wait_ge(sem, val)
            self.seen[(e, sid)] = val
            self.n_inst += 1
        if held is not None:
            sid, (sem, val) = held
            self.seen[(e, sid)] = val
            return (sem, val)
        return None

    def _record(self, ticket, reads, writes):
        for k in reads:
            self.readers.setdefault(k, []).append(ticket)
        for k in writes:
            self.last_w[k] = ticket
            self.readers[k] = []

    def op(self, e, fn, reads=(), writes=()):
        need = self._need(reads, writes)
        held = self._emit_waits(e, need, is_pe_mm=(e == 'pe'), embed=self.EMBED)
        if e not in self.csem or self.ccnt[e] >= self.EPOCH:
            self.csem[e] = self.new_sem(f"c{e}")
            self.ccnt[e] = 0
        ins = fn()
        if held is not None:
            ins._wait_ge(held[0], held[1])
        self.ccnt[e] += 1
        ins.then_inc(self.csem[e], 1)
        self.n_inst += 1
        ticket = (self.csem[e], self.ccnt[e], e)
        self._record(ticket, reads, writes)
        return ticket

    def dma(self, q, out, in_, reads=(), writes=(), **kw):
        need = self._need(reads, writes)
        if q not in self.slots:
            self.slots[q] = [[self.new_sem(f"d{q}"), 0] for _ in range(self.NSLOT)]
            self.slot_rr[q] = 0
        i = self.slot_rr[q]
        self.slot_rr[q] = (i + 1) % self.NSLOT
        slot = self.slots[q][i]
        if slot[1] >= 16 * 3000:
            slot[0] = self.new_sem(f"d{q}")
            slot[1] = 0
        if slot[1] > 0:
            need.append((slot[0], slot[1], 'dma'))
        held = self._emit_waits(q, need, embed=self.EMBED)
        ins = self.eng[q].dma_start(out=out, in_=in_, **kw)
        if held is not None:
            ins._wait_ge(held[0], held[1])
        slot[1] += 16
        ins.then_inc(slot[0], 16)
        self.n_inst += 1
        ticket = (slot[0], slot[1], 'dma')
        self._record(ticket, reads, writes)
        return ticket

    def finish(self, e='sp'):
        need = []
        for q, sl in self.slots.items():
            for sem, cnt in sl:
                if cnt > 0:
                    need.append((sem, cnt, 'dma'))
        for k, t in self.last_w.items():
            need.append(t)
        self._emit_waits(e, need)

    def close(self):
        self.es.close()
D = 4096
KC_D = 32
FH = 11008
EPS = 1e-6


class WPool:
    def __init__(self, kb, nbuf=4, kc=16):
        self.kb = kb
        self.kc = kc
        self.tiles = [kb.sb(f"wp{i}", [128, kc, 128]) for i in range(nbuf)]
        self.i = 0

    def get(self):
        t = self.tiles[self.i]
        k = f"wp{self.i}"
        self.i = (self.i + 1) % len(self.tiles)
        return t, k


class PsPool:
    def __init__(self, kb, n=8):
        self.tiles = [kb.ps(f"psb{i}", [128, 512]) for i in range(n)]
        self.i = 0
        self.n = n

    def get(self):
        t = self.tiles[self.i]
        k = f"psb{self.i}"
        self.i = (self.i + 1) % self.n
        return t, k


def linear(kb, wp, ps_ap, ps_key, w2d, col0, ncols, rhs_fn, rhs_key, KC, q='sp'):
    nc = kb.nc
    K = w2d.shape[0]
    g = 0
    while g < KC:
        n = min(wp.kc, KC - g)
        wt, wk = wp.get()
        r0 = g * 128
        r1 = min(K, (g + n) * 128)
        nfull = (r1 - r0) // 128
        if nfull > 0:
            src = w2d[r0:r0 + nfull * 128, col0:col0 + ncols].rearrange("(c p) n -> p c n", p=128)
            kb.dma(q, wt[:, 0:nfull, 0:ncols], src, writes=[wk])
        rem = (r1 - r0) - nfull * 128
        if rem > 0:
            kb.dma(q, wt[0:rem, nfull, 0:ncols], w2d[r0 + nfull * 128:r1, col0:col0 + ncols], writes=[wk])
        for c in range(n):
            kk = 128
            if (g + c + 1) * 128 > K:
                kk = K - (g + c) * 128
            rhs = rhs_fn(g + c)
            kb.op('pe', lambda wt=wt, c=c, rhs=rhs, kk=kk, first=(g + c == 0), last=(g + c == KC - 1):
                  nc.tensor.matmul(ps_ap, wt[0:kk, c, 0:ncols], rhs[0:kk], start=first, stop=last),
                  reads=[wk, rhs_key], writes=[ps_key])
        g += n
def build_stage0():
    nc = bass.Bass("TRN2", target_bir_lowering=False)
    kb = KB(nc)
    NB = 48
    aw = nc.dram_tensor("aw", [D, NB * 128], F32, kind="ExternalInput")
    ab = nc.dram_tensor("ab", [128, NB], F32, kind="ExternalInput")
    ct = nc.dram_tensor("ct", [128, KC_D, 3], F32, kind="ExternalInput")
    mo = nc.dram_tensor("mo", [128, NB, 3], F32, kind="ExternalOutput")
    wp = WPool(kb, 4, 16)
    pp = PsPool(kb, 4)
    cts = kb.sb("cts", [128, KC_D, 3]); sil = kb.sb("sil", [128, KC_D, 3])
    abt = kb.sb("abt", [128, NB]); res = kb.sb("res", [128, NB, 3])
    kb.dma('pool', cts[:], ct.ap(), writes=['cts'])
    kb.dma('pool', abt[:], ab.ap(), writes=['abt'])
    kb.op('act', lambda: nc.scalar.activation(sil[:], cts[:], AF.Silu), reads=['cts'], writes=['sil'])
    for blk in range(NB):
        ps, pk = pp.get()
        linear(kb, wp, ps[:, 0:3], pk, aw.ap(), blk * 128, 128, lambda c: sil[:, c, :], 'sil', KC_D)
        kb.op('act', lambda ps=ps, blk=blk: nc.scalar.activation(res[:, blk, :], ps[:, 0:3], AF.Identity, bias=abt[:, blk:blk + 1]),
              reads=[pk, 'abt'], writes=['res'])
    kb.dma('pool', mo.ap(), res[:], reads=['res'])
    kb.finish('sp')
    kb.close()
    return nc


def rstd_of(kb, nc, pp, xsrc, xkey, sq, sqkey, ones, n, rstd, rkey, kc=KC_D, dim=D):
    kb.op('act', lambda: nc.scalar.activation(sq[:, 0:kc, 0:n], xsrc[:, 0:kc, 0:n], AF.Square), reads=[xkey], writes=[sqkey])
    ps, pk = pp.get()
    for c in range(kc):
        kb.op('pe', lambda c=c: nc.tensor.matmul(ps[:, 0:n], ones[:], sq[:, c, 0:n], start=(c == 0), stop=(c == kc - 1)),
              reads=[sqkey, 'ones'], writes=[pk])
    kb.op('dve', lambda: nc.vector.tensor_scalar(rstd[:, 0:n], ps[:, 0:n], 1.0 / dim, EPS, ALU.mult, ALU.add), reads=[pk], writes=[rkey])
    kb.op('act', lambda: nc.scalar.sqrt(rstd[:, 0:n], rstd[:, 0:n]), reads=[rkey], writes=[rkey])
    kb.op('dve', lambda: nc.vector.reciprocal(rstd[:, 0:n], rstd[:, 0:n]), reads=[rkey], writes=[rkey])


def modulate(kb, nc, x, xkey, xm, xmkey, rstd, rkey, A, shift, vkey, n, kc=KC_D):
    rb = rstd[:, 0:n].unsqueeze(1).broadcast_to([128, kc, n])
    kb.op('dve', lambda: nc.vector.tensor_tensor(xm[:, 0:kc, 0:n], x[:, 0:kc, 0:n], rb, ALU.mult), reads=[xkey, rkey], writes=[xmkey])
    ab = A.unsqueeze(2).broadcast_to([128, kc, n])
    kb.op('dve', lambda: nc.vector.tensor_tensor(xm[:, 0:kc, 0:n], xm[:, 0:kc, 0:n], ab, ALU.mult), reads=[xmkey, vkey], writes=[xmkey])
    if shift is not None:
        sb_ = shift.unsqueeze(2).broadcast_to([128, kc, n])
        kb.op('gp' if False else 'dve', lambda: nc.vector.tensor_tensor(xm[:, 0:kc, 0:n], xm[:, 0:kc, 0:n], sb_, ALU.add), reads=[xmkey, vkey], writes=[xmkey])


def load_fm(kb, q, dst, dkey, src2d, t0, n, kc, step=8):
    for c0 in range(0, kc, step):
        c1 = min(kc, c0 + step)
        src = src2d[c0 * 128:c1 * 128, t0:t0 + n].rearrange("(c p) t -> p c t", p=128)
        kb.dma(q, dst[:, c0:c1, 0:n], src, writes=[dkey])


def store_fm(kb, q, dst2d, t0, n, src, skey, kc, step=8):
    for c0 in range(0, kc, step):
        c1 = min(kc, c0 + step)
        d = dst2d[c0 * 128:c1 * 128, t0:t0 + n].rearrange("(c p) t -> p c t", p=128)
        kb.dma(q, d, src[:, c0:c1, 0:n], reads=[skey])


def build_stage2(n_ctx, n_lat, last, NT=256):
    nc = bass.Bass("TRN2", target_bir_lowering=False)
    kb = KB(nc)
    ntok = n_ctx + n_lat
    HC = FH // 128
    xT = nc.dram_tensor("xT", [D, ntok], F32, kind="ExternalInput")
    oT = nc.dram_tensor("oT", [3072, ntok], F32, kind="ExternalInput")
    wgi = nc.dram_tensor("wgi", [D, 3 * D], F32, kind="ExternalInput")
    wbr = nc.dram_tensor("wbr", [3072, D], F32, kind="ExternalInput")
    wo = nc.dram_tensor("wo", [D, D], F32, kind="ExternalInput")
    fg = nc.dram_tensor("fg", [D, FH], F32, kind="ExternalInput")
    fu = nc.dram_tensor("fu", [D, FH], F32, kind="ExternalInput")
    fd = nc.dram_tensor("fd", [FH, D], F32, kind="ExternalInput")
    vecs = nc.dram_tensor("vecs", [128, 15, KC_D], F32, kind="ExternalInput")
    ones_d = nc.dram_tensor("ones", [128, 128], F32, kind="ExternalInput")
    xo = nc.dram_tensor("xo", [D, ntok], F32, kind="ExternalOutput")
    obf = nc.dram_tensor("obf", [1024, ntok], F32, kind="ExternalInput")
    obb = nc.dram_tensor("obb", [1024, ntok], F32, kind="ExternalInput")
    ogg = nc.dram_tensor("ogg", [1024, ntok], F32, kind="ExternalInput")
    hgn_d = nc.dram_tensor("hgn", [128, 1], F32, kind="ExternalInput")
    ryf = nc.dram_tensor("ryf", [1024, ntok], F32, kind="ExternalInput")
    ryb = nc.dram_tensor("ryb", [1024, ntok], F32, kind="ExternalInput")
    rgg = nc.dram_tensor("rgg", [1024, ntok], F32, kind="ExternalInput")
    rbn = nc.dram_tensor("rbn", [1024, ntok], F32, kind="ExternalInput")
    rln_d = nc.dram_tensor("rln", [128, 8, 2], F32, kind="ExternalInput")
    bones_d = nc.dram_tensor("bones", [128, 128], F32, kind="ExternalInput")
    wp = WPool(kb, 4, 16)
    pp = PsPool(kb, 8)
    x = kb.sb("x", [128, KC_D, NT]); xm = kb.sb("xm", [128, KC_D, NT]); h = kb.sb("h", [128, HC, NT])
    vt = kb.sb("vt", [128, 15, KC_D]); ones = kb.sb("ones", [128, 128])
    co = kb.sb("co", [128, 6, KC_D])
    rstd = kb.sb("rstd", [128, NT]); gsb = kb.sb("gsb", [128, NT]); tmp = kb.sb("tmp", [128, NT])
    hgn = kb.sb("hgn", [128, 1]); rln = kb.sb("rln", [128, 8, 2]); bones = kb.sb("bones", [128, 128])
    kb.dma('pool', hgn[:], hgn_d.ap(), writes=['hgn'])
    kb.dma('pool', rln[:], rln_d.ap(), writes=['rln'])
    kb.dma('pool', bones[:], bones_d.ap(), writes=['bones'])
    kb.dma('pool', vt[:], vecs.ap(), writes=['vt'])
    kb.dma('pool', ones[:], ones_d.ap(), writes=['ones'])
    for i, (gidx, sidx) in enumerate([(12, 1), (12, 7), (13, 4), (13, 10)]):
        kb.op('dve', lambda i=i, sidx=sidx: nc.vector.tensor_scalar_add(co[:, i, :], vt[:, sidx, :], 1.0), reads=['vt'], writes=['co'])
        kb.op('dve', lambda i=i, gidx=gidx: nc.vector.tensor_tensor(co[:, i, :], co[:, i, :], vt[:, gidx, :], ALU.mult), reads=['vt', 'co'], writes=['co'])
    tiles = []
    if n_ctx:
        tiles.append((0, n_ctx, True))
    for t0 in range(0, n_lat, NT):
        tiles.append((n_ctx + t0, min(NT, n_lat - t0), False))
    for (t0, n, is_ctx) in tiles:
        mb = 6 if is_ctx else 0
        ci = 1 if is_ctx else 0
        load_fm(kb, 'pool', x, 'x', xT.ap(), t0, n, KC_D)
        load_fm(kb, 'pool', h[:, 32:56, :], 'h', oT.ap(), t0, n, 24)
        load_fm(kb, 'pool', xm[:, 0:8, :], 'xm', ryf.ap(), t0, n, 8)
        load_fm(kb, 'pool', xm[:, 8:16, :], 'xm', ryb.ap(), t0, n, 8)
        load_fm(kb, 'pool', xm[:, 16:24, :], 'xm', rgg.ap(), t0, n, 8)
        load_fm(kb, 'pool', xm[:, 24:32, :], 'xm', rbn.ap(), t0, n, 8)
        kb.op('dve', lambda: nc.vector.tensor_tensor(xm[:, 0:8, 0:n], xm[:, 0:8, 0:n], xm[:, 8:16, 0:n], ALU.add), reads=['xm'], writes=['xm'])
        for c in range(8):
            ps, pk = pp.get()
            kb.op('pe', lambda ps=ps, c=c: nc.tensor.matmul(ps[:, 0:n], bones[:], xm[:, c, 0:n], start=True, stop=True), reads=['xm', 'bones'], writes=[pk])
            kb.op('dve', lambda ps=ps, c=c: nc.vector.scalar_tensor_tensor(gsb[:, 0:n], ps[:, 0:n], -1.0 / 64, xm[:, c, 0:n], ALU.mult, ALU.add),
                  reads=[pk, 'xm'], writes=['gsb'])
            kb.op('act', lambda: nc.scalar.activation(tmp[:, 0:n], gsb[:, 0:n], AF.Square), reads=['gsb'], writes=['tmp'])
            ps2, pk2 = pp.get()
            kb.op('pe', lambda ps2=ps2: nc.tensor.matmul(ps2[:, 0:n], bones[:], tmp[:, 0:n], start=True, stop=True), reads=['tmp', 'bones'], writes=[pk2])
            kb.op('dve', lambda ps2=ps2: nc.vector.tensor_scalar(rstd[:, 0:n], ps2[:, 0:n], 1.0 / 64, 64e-5, ALU.mult, ALU.add), reads=[pk2], writes=['rstd'])
            kb.op('act', lambda: nc.scalar.sqrt(rstd[:, 0:n], rstd[:, 0:n]), reads=['rstd'], writes=['rstd'])
            kb.op('dve', lambda: nc.vector.reciprocal(rstd[:, 0:n], rstd[:, 0:n]), reads=['rstd'], writes=['rstd'])
            kb.op('dve', lambda c=c: nc.vector.tensor_tensor(h[:, 32 + c, 0:n], gsb[:, 0:n], rstd[:, 0:n], ALU.mult), reads=['gsb', 'rstd'], writes=['h'])
            kb.op('dve', lambda c=c: nc.vector.tensor_scalar(h[:, 32 + c, 0:n], h[:, 32 + c, 0:n], rln[:, c, 0:1], rln[:, c, 1:2], ALU.mult, ALU.add), reads=['rln', 'h'], writes=['h'])
            kb.op('dve', lambda c=c: nc.vector.tensor_tensor(h[:, 32 + c, 0:n], h[:, 32 + c, 0:n], xm[:, 24 + c, 0:n], ALU.add), reads=['xm', 'h'], writes=['h'])
            kb.op('dve', lambda c=c: nc.vector.tensor_tensor(h[:, 32 + c, 0:n], h[:, 32 + c, 0:n], xm[:, 16 + c, 0:n], ALU.mult), reads=['xm', 'h'], writes=['h'])
        load_fm(kb, 'pool', xm[:, 0:8, :], 'xm', obf.ap(), t0, n, 8)
        load_fm(kb, 'pool', xm[:, 8:16, :], 'xm', obb.ap(), t0, n, 8)
        load_fm(kb, 'pool', xm[:, 16:24, :], 'xm', ogg.ap(), t0, n, 8)
        kb.op('dve', lambda: nc.vector.tensor_tensor(xm[:, 0:8, 0:n], xm[:, 0:8, 0:n], xm[:, 8:16, 0:n], ALU.add), reads=['xm'], writes=['xm'])
        kb.op('act', lambda: nc.scalar.activation(xm[:, 24:32, 0:n], xm[:, 0:8, 0:n], AF.Square), reads=['xm'], writes=['xm'])
        kb.op('act', lambda: nc.scalar.activation(xm[:, 16:24, 0:n], xm[:, 16:24, 0:n], AF.Silu), reads=['xm'], writes=['xm'])
        for c in range(8):
            ps, pk = pp.get()
            kb.op('pe', lambda ps=ps, c=c: nc.tensor.matmul(ps[:, 0:n], ones[:], xm[:, 24 + c, 0:n], start=True, stop=True), reads=['xm', 'ones'], writes=[pk])
            kb.op('dve', lambda ps=ps: nc.vector.tensor_scalar(rstd[:, 0:n], ps[:, 0:n], 1.0 / 128, EPS, ALU.mult, ALU.add), reads=[pk], writes=['rstd'])
            kb.op('act', lambda: nc.scalar.sqrt(rstd[:, 0:n], rstd[:, 0:n]), reads=['rstd'], writes=['rstd'])
            kb.op('dve', lambda: nc.vector.reciprocal(rstd[:, 0:n], rstd[:, 0:n]), reads=['rstd'], writes=['rstd'])
            kb.op('dve', lambda c=c: nc.vector.tensor_tensor(h[:, 40 + c, 0:n], xm[:, c, 0:n], rstd[:, 0:n], ALU.mult), reads=['xm', 'rstd'], writes=['h'])
            kb.op('dve', lambda c=c: nc.vector.tensor_tensor(h[:, 40 + c, 0:n], h[:, 40 + c, 0:n], xm[:, 16 + c, 0:n], ALU.mult), reads=['xm', 'h'], writes=['h'])
            kb.op('dve', lambda c=c: nc.vector.tensor_scalar_mul(h[:, 40 + c, 0:n], h[:, 40 + c, 0:n], hgn[:, 0:1]), reads=['hgn', 'h'], writes=['h'])
        rstd_of(kb, nc, pp, x, 'x', xm, 'xm', ones, n, rstd, 'rstd')
        modulate(kb, nc, x, 'x', xm, 'xm', rstd, 'rstd', co[:, ci, :], vt[:, mb + 0, :], 'co', n)
        for nb in range(KC_D):
            for j in range(3):
                psg, kg = pp.get()
                linear(kb, wp, psg[:, 0:n], kg, wgi.ap(), j * D + nb * 128, 128, lambda c: xm[:, c, 0:n], 'xm', KC_D)
                kb.op('act', lambda psg=psg: nc.scalar.activation(gsb[:, 0:n], psg[:, 0:n], AF.Sigmoid), reads=[kg], writes=['gsb'])
                psb, kbk = pp.get()
                linear(kb, wp, psb[:, 0:n], kbk, wbr.ap()[j * 1024:(j + 1) * 1024, :], nb * 128, 128,
                       lambda c, j=j: h[:, 32 + j * 8 + c, 0:n], 'h', 8)
                if j == 0:
                    kb.op('dve', lambda psb=psb, nb=nb: nc.vector.tensor_tensor(h[:, nb, 0:n], gsb[:, 0:n], psb[:, 0:n], ALU.mult),
                          reads=['gsb', kbk], writes=['y'])
                else:
                    kb.op('dve', lambda psb=psb: nc.vector.tensor_tensor(tmp[:, 0:n], gsb[:, 0:n], psb[:, 0:n], ALU.mult),
                          reads=['gsb', kbk], writes=['tmp'])
                    kb.op('dve', lambda nb=nb: nc.vector.tensor_tensor(h[:, nb, 0:n], h[:, nb, 0:n], tmp[:, 0:n], ALU.add),
                          reads=['tmp', 'y'], writes=['y'])
        for nb in range(KC_D):
            ps, pk = pp.get()
            linear(kb, wp, ps[:, 0:n], pk, wo.ap(), nb * 128, 128, lambda c: h[:, c, 0:n], 'y', KC_D)
            kb.op('dve', lambda ps=ps, nb=nb: nc.vector.scalar_tensor_tensor(x[:, nb, 0:n], ps[:, 0:n], vt[:, mb + 2, nb:nb + 1], x[:, nb, 0:n], ALU.mult, ALU.add),
                  reads=[pk, 'vt', 'x'], writes=['x'])
        rstd_of(kb, nc, pp, x, 'x', xm, 'xm', ones, n, rstd, 'rstd')
        modulate(kb, nc, x, 'x', xm, 'xm', rstd, 'rstd', co[:, 2 + ci, :], vt[:, mb + 3, :], 'co', n)
        for m in range(HC):
            psg, kg = pp.get()
            linear(kb, wp, psg[:, 0:n], kg, fg.ap(), m * 128, 128, lambda c: xm[:, c, 0:n], 'xm', KC_D)
            psu, ku = pp.get()
            linear(kb, wp, psu[:, 0:n], ku, fu.ap(), m * 128, 128, lambda c: xm[:, c, 0:n], 'xm', KC_D)
            kb.op('act', lambda psg=psg: nc.scalar.activation(gsb[:, 0:n], psg[:, 0:n], AF.Silu), reads=[kg], writes=['gsb'])
            kb.op('dve', lambda psu=psu, m=m: nc.vector.tensor_tensor(h[:, m, 0:n], gsb[:, 0:n], psu[:, 0:n], ALU.mult),
                  reads=['gsb', ku], writes=['h', 'y'])
        for nb in range(KC_D):
            ps, pk = pp.get()
            linear(kb, wp, ps[:, 0:n], pk, fd.ap(), nb * 128, 128, lambda c: h[:, c, 0:n], 'h', HC)
            kb.op('dve', lambda ps=ps, nb=nb: nc.vector.scalar_tensor_tensor(x[:, nb, 0:n], ps[:, 0:n], vt[:, mb + 5, nb:nb + 1], x[:, nb, 0:n], ALU.mult, ALU.add),
                  reads=[pk, 'vt', 'x'], writes=['x'])
        if last:
            rstd_of(kb, nc, pp, x, 'x', xm, 'xm', ones, n, rstd, 'rstd')
            modulate(kb, nc, x, 'x', xm, 'xm', rstd, 'rstd', vt[:, 14, :], None, 'vt', n)
            store_fm(kb, 'pool', xo.ap(), t0, n, xm, 'xm', KC_D)
        else:
            store_fm(kb, 'pool', xo.ap(), t0, n, x, 'x', KC_D)
    kb.finish('sp')
    kb.close()
    return nc
NCOL1 = 3424
R_HG = 1280
R_HY = 2560
R_GD = 3328
TOK_T = 512


def seg_tiles(n_ctx, n_lat, nt=TOK_T):
    t = []
    for t0 in range(0, n_ctx, nt):
        t.append((t0, min(nt, n_ctx - t0), True))
    for t0 in range(0, n_lat, nt):
        t.append((n_ctx + t0, min(nt, n_lat - t0), False))
    return t


def build_stage1a(n_ctx, n_lat, L):
    nc = bass.Bass("TRN2", target_bir_lowering=False)
    kb = KB(nc)
    T = n_ctx + n_lat
    xT = nc.dram_tensor("xT", [D, T], F32, kind="ExternalInput")
    w1 = nc.dram_tensor("w1", [D, NCOL1], F32, kind="ExternalInput")
    vecs = nc.dram_tensor("vecs", [128, 5, KC_D], F32, kind="ExternalInput")
    ones_d = nc.dram_tensor("ones", [128, 128], F32, kind="ExternalInput")
    pT = nc.dram_tensor("pT", [NCOL1, T], F32, kind="ExternalOutput")
    wp = WPool(kb, 4, 16)
    pp = PsPool(kb, 8)
    NT = TOK_T
    x = kb.sb("x", [128, KC_D, NT]); xm = kb.sb("xm", [128, KC_D, NT])
    vt = kb.sb("vt", [128, 5, KC_D]); ones = kb.sb("ones", [128, 128]); co = kb.sb("co", [128, 2, KC_D])
    rstd = kb.sb("rstd", [128, NT])
    ev = [kb.sb(f"ev{i}", [128, NT]) for i in range(3)]
    kb.dma('pool', vt[:], vecs.ap(), writes=['vt'])
    kb.dma('pool', ones[:], ones_d.ap(), writes=['ones'])
    for i, sidx in enumerate([1, 3]):
        kb.op('dve', lambda i=i, sidx=sidx: nc.vector.tensor_scalar_add(co[:, i, :], vt[:, sidx, :], 1.0), reads=['vt'], writes=['co'])
        kb.op('dve', lambda i=i: nc.vector.tensor_tensor(co[:, i, :], co[:, i, :], vt[:, 4, :], ALU.mult), reads=['vt', 'co'], writes=['co'])
    NB = (NCOL1 + 127) // 128
    ei = 0
    for (t0, n, is_ctx) in seg_tiles(n_ctx, n_lat):
        ci = 1 if is_ctx else 0
        load_fm(kb, 'pool', x, 'x', xT.ap(), t0, n, KC_D)
        rstd_of(kb, nc, pp, x, 'x', xm, 'xm', ones, n, rstd, 'rstd')
        modulate(kb, nc, x, 'x', xm, 'xm', rstd, 'rstd', co[:, ci, :], vt[:, 2 * ci, :], 'co', n)
        for nb in range(NB):
            ncols = min(128, NCOL1 - nb * 128)
            ps, pk = pp.get()
            linear(kb, wp, ps[0:ncols, 0:n], pk, w1.ap(), nb * 128, ncols, lambda c: xm[:, c, 0:n], 'xm', KC_D)
            e = ev[ei % 3]; ek = f"ev{ei % 3}"; ei += 1
            kb.op('act', lambda e=e, ps=ps, ncols=ncols: nc.scalar.copy(e[0:ncols, 0:n], ps[0:ncols, 0:n]), reads=[pk], writes=[ek])
            kb.dma('pool', pT.ap()[nb * 128:nb * 128 + ncols, t0:t0 + n], e[0:ncols, 0:n], reads=[ek])
    kb.finish('sp')
    kb.close()
    return nc
def build_hg(T, L, NH=2):
    nc = bass.Bass("TRN2", target_bir_lowering=False)
    kb = KB(nc)
    TC = T // NH
    hq = nc.dram_tensor("hq", [2, 2, 128, T], F32, kind="ExternalInput")
    hz = nc.dram_tensor("hz", [2, 2, 128, T], F32, kind="ExternalInput")
    hi = nc.dram_tensor("hi", [2, 2, 128, T], F32, kind="ExternalInput")
    hgl = nc.dram_tensor("hgl", [128, 2, 2, 2], F32, kind="ExternalInput")
    ones_d = nc.dram_tensor("ones", [128, 128], F32, kind="ExternalInput")
    ho = nc.dram_tensor("ho", [2, 2, 128, T], F32, kind="ExternalOutput")
    pp = PsPool(kb, 8)
    fT = kb.sb("fT", [128, TC]); kT = kb.sb("kT", [128, TC]); qT = kb.sb("qT", [128, TC])
    ib = [kb.sb(f"ib{i}", [128, TC]) for i in range(2)]
    S = kb.sb("S", [128, TC])
    carry = kb.sb("carry", [128, 128])
    ones = kb.sb("ones", [128, 128])
    orow = [kb.sb(f"orow{i}", [1, 2048]) for i in range(2)]
    lt = kb.sb("lt", [128, 2, 2, 2]); lv = kb.sb("lv", [128, 2, 2]); oml = kb.sb("oml", [128, 2, 2])
    kb.dma('pool', ones[:], ones_d.ap(), writes=['ones'])
    kb.dma('pool', lt[:], hgl.ap(), writes=['lt'])
    if L == 0:
        kb.op('dve', lambda: nc.vector.memset(lv[:], 0.0), writes=['lv'])
    else:
        kb.op('dve', lambda: nc.vector.tensor_tensor(lv[:], lt[:, 1, :, :], lt[:, 0, :, :], ALU.subtract), reads=['lt'], writes=['lv'])
        kb.op('act', lambda: nc.scalar.activation(lv[:], lv[:], AF.Sigmoid), reads=['lv'], writes=['lv'])
    kb.op('dve', lambda: nc.vector.tensor_scalar(oml[:], lv[:], -1.0, 1.0, ALU.mult, ALU.add), reads=['lv'], writes=['oml'])
    oi = 0
    for d in range(2):
        for hh in range(2):
            kb.op('dve', lambda: nc.vector.memset(carry[:], 0.0), writes=['carry'])
            for half in range(NH):
                c0 = half * TC
                kb.dma('pool', fT[:], hz.ap()[d, hh, :, c0:c0 + TC], writes=['fT'])
                kb.dma('pool', qT[:], hq.ap()[d, hh, :, c0:c0 + TC], writes=['qT'])
                kb.op('act', lambda: nc.scalar.activation(fT[:], fT[:], AF.Sigmoid), reads=['fT'], writes=['fT'])
                kb.op('dve', lambda d=d, hh=hh: nc.vector.tensor_scalar(fT[:], fT[:], oml[:, d, hh:hh + 1], lv[:, d, hh:hh + 1], ALU.mult, ALU.add),
                      reads=['fT', 'oml', 'lv'], writes=['fT'])
                kb.op('dve', lambda: nc.vector.tensor_scalar(kT[:], fT[:], -1.0, 1.0, ALU.mult, ALU.add), reads=['fT'], writes=['kT'])
                for v in range(128):
                    b_ = ib[v % 2]; bk = f"ib{v % 2}"
                    src = bass.AP(tensor=hi, offset=((d * 2 + hh) * 128 + v) * T + c0, ap=[[0, 128], [1, TC]])
                    kb.dma('sp', b_[:], src, writes=[bk])
                    kb.op('dve', lambda b_=b_: nc.vector.tensor_tensor(b_[:], b_[:], kT[:], ALU.mult), reads=[bk, 'kT'], writes=[bk])
                    kb.op('dve', lambda b_=b_, v=v: nc.vector.tensor_tensor_scan(S[:], fT[:], b_[:], carry[:, v:v + 1], ALU.mult, ALU.add),
                          reads=[bk, 'fT', 'carry'], writes=['S'])
                    kb.op('dve', lambda v=v: nc.vector.tensor_copy(carry[:, v:v + 1], S[:, TC - 1:TC]), reads=['S'], writes=['carry'])
                    kb.op('dve', lambda: nc.vector.tensor_tensor(S[:], S[:], qT[:], ALU.mult), reads=['S', 'qT'], writes=['S'])
                    for g0 in range(0, TC, 2048):
                        g1 = min(TC, g0 + 2048)
                        orw = orow[oi % 2]; ok_ = f"orow{oi % 2}"; oi += 1
                        for s0 in range(g0, g1, 512):
                            s1 = min(g1, s0 + 512)
                            ps, pk = pp.get()
                            kb.op('pe', lambda ps=ps, s0=s0, s1=s1: nc.tensor.matmul(ps[0:1, 0:s1 - s0], ones[:, 0:1], S[:, s0:s1], start=True, stop=True),
                                  reads=['S', 'ones'], writes=[pk])
                            kb.op('act', lambda ps=ps, orw=orw, s0=s0, s1=s1, g0=g0: nc.scalar.copy(orw[0:1, s0 - g0:s1 - g0], ps[0:1, 0:s1 - s0]),
                                  reads=[pk], writes=[ok_])
                        kb.dma('pool', ho.ap()[d, hh, v:v + 1, c0 + g0:c0 + g1], orw[0:1, 0:g1 - g0], reads=[ok_])
    kb.finish('sp')
    kb.close()
    return nc
import math
PI = math.pi


def build_hy(segs, NMAX):
    nc = bass.Bass("TRN2", target_bir_lowering=False)
    kb = KB(nc)
    T = sum(n for _, n in segs)
    hp = nc.dram_tensor("hp", [768, T], F32, kind="ExternalInput")
    cw = nc.dram_tensor("cw", [128, 6, 4], F32, kind="ExternalInput")
    fw1 = nc.dram_tensor("fw1", [33, 64], F32, kind="ExternalInput")
    fw2 = nc.dram_tensor("fw2", [64, 64], F32, kind="ExternalInput")
    fw3 = nc.dram_tensor("fw3", [64, 512], F32, kind="ExternalInput")
    fv = nc.dram_tensor("fv", [64, 4], F32, kind="ExternalInput")
    hb = nc.dram_tensor("hb", [128, 2], F32, kind="ExternalInput")
    ztab = nc.dram_tensor("ztab", [2, 33, T], F32, kind="ExternalInput")
    wtab = nc.dram_tensor("wtab", [2, 2, 128, T], F32, kind="ExternalInput")
    hyo = nc.dram_tensor("hyo", [256, T], F32, kind="ExternalOutput")
    pp = PsPool(kb, 8)
    G = kb.sb("G", [128, 2 * NMAX]); Z = kb.sb("Z", [128, NMAX]); A1 = kb.sb("A1", [128, NMAX]); A2 = kb.sb("A2", [128, NMAX])
    cwt = kb.sb("cwt", [128, 6, 4]); w1t = kb.sb("w1t", [33, 64]); w2t = kb.sb("w2t", [64, 64]); w3t = kb.sb("w3t", [64, 512])
    fvt = kb.sb("fvt", [64, 4]); hbt = kb.sb("hbt", [128, 2])
    zt = kb.sb("zt", [33, 512]); h1 = kb.sb("h1", [64, 512]); h2 = kb.sb("h2", [64, 512]); wt_ = kb.sb("wt_", [128, 512])
    sm = kb.sb("sm", [128, 40])
    for dst, src, k in [(cwt, cw, 'cwt'), (w1t, fw1, 'w1t'), (w2t, fw2, 'w2t'), (w3t, fw3, 'w3t'), (fvt, fv, 'fvt'), (hbt, hb, 'hbt')]:
        kb.dma('pool', dst[:], src.ap(), writes=[k])

    def sconv(dst, dk, src, sk, ci, n, eng='dve'):
        e = nc.vector
        kb.op('dve', lambda: e.tensor_scalar(dst[:, 0:n], src[:, 0:n], cwt[:, ci, 1:2], cwt[:, ci, 3:4], ALU.mult, ALU.add), reads=[sk, 'cwt'], writes=[dk])
        kb.op('dve', lambda: e.scalar_tensor_tensor(dst[:, 1:n], src[:, 0:n - 1], cwt[:, ci, 0:1], dst[:, 1:n], ALU.mult, ALU.add), reads=[sk, 'cwt', dk], writes=[dk])
        kb.op('dve', lambda: e.scalar_tensor_tensor(dst[:, 0:n - 1], src[:, 1:n], cwt[:, ci, 2:3], dst[:, 0:n - 1], ALU.mult, ALU.add), reads=[sk, 'cwt', dk], writes=[dk])

    yi = kb.sb("yi", [64, 512], mybir.dt.int32); yf = kb.sb("yf", [64, 512]); s2 = kb.sb("s2", [64, 512])

    def sin_of(ps, pk, bcol, out, ok_, m):
        V = nc.vector
        kb.op('dve', lambda: V.tensor_scalar(out[:, 0:m], ps, fvt[:, bcol:bcol + 1], fvt[:, 2:3], ALU.add, ALU.mult), reads=[pk, 'fvt'], writes=[ok_])
        kb.op('dve', lambda: V.tensor_scalar(out[:, 0:m], out[:, 0:m], 1.0 / (2 * PI), 8.5, ALU.mult, ALU.add), reads=[ok_], writes=[ok_])
        kb.op('dve', lambda: V.tensor_copy(yi[:, 0:m], out[:, 0:m]), reads=[ok_], writes=['yi'])
        kb.op('dve', lambda: V.tensor_copy(yf[:, 0:m], yi[:, 0:m]), reads=['yi'], writes=['yf'])
        kb.op('dve', lambda: V.tensor_tensor(yf[:, 0:m], out[:, 0:m], yf[:, 0:m], ALU.subtract), reads=[ok_, 'yf'], writes=['yf'])
        kb.op('act', lambda: nc.scalar.activation(out[:, 0:m], yf[:, 0:m], AF.Sin, scale=PI), reads=['yf'], writes=[ok_])
        kb.op('act', lambda: nc.scalar.activation(s2[:, 0:m], yf[:, 0:m], AF.Sin, scale=PI / 2), reads=['yf'], writes=['s2'])
        kb.op('dve', lambda: V.tensor_tensor(s2[:, 0:m], s2[:, 0:m], s2[:, 0:m], ALU.mult), reads=['s2'], writes=['s2'])
        kb.op('dve', lambda: V.tensor_scalar(s2[:, 0:m], s2[:, 0:m], 4.0, -2.0, ALU.mult, ALU.add), reads=['s2'], writes=['s2'])
        kb.op('dve', lambda: V.tensor_tensor(out[:, 0:m], out[:, 0:m], s2[:, 0:m], ALU.mult), reads=[ok_, 's2'], writes=[ok_])

    for (t0, n) in segs:
        for j in range(2):
            kb.dma('pool', G[:, 0:n], hp.ap()[256 + j * 128:256 + (j + 1) * 128, t0:t0 + n], writes=['G'])
            kb.dma('pool', G[:, NMAX:NMAX + n], hp.ap()[512 + j * 128:512 + (j + 1) * 128, t0:t0 + n], writes=['G'])
            sconv(Z, 'Z', G, 'G', 2 + j, n)
            sconv(A1, 'A1', G[:, NMAX:], 'G', 4 + j, n)
            kb.op('dve', lambda: nc.vector.tensor_tensor(Z[:, 0:n], Z[:, 0:n], A1[:, 0:n], ALU.mult), reads=['Z', 'A1'], writes=['Z'])
            for which in (1, 0):
                for p0 in range(0, n, 512):
                    m = min(512, n - p0)
                    kb.dma('sp', zt[:, 0:m], ztab.ap()[which, :, t0 + p0:t0 + p0 + m], writes=['zt'])
                    kb.dma('sp', wt_[:, 0:m], wtab.ap()[which, j, :, t0 + p0:t0 + p0 + m], writes=['wt_'])
                    ps, pk = pp.get()
                    kb.op('pe', lambda ps=ps, m=m: nc.tensor.matmul(ps[0:64, 0:m], w1t[:, :], zt[:, 0:m], start=True, stop=True), reads=['w1t', 'zt'], writes=[pk])
                    sin_of(ps[0:64, 0:m], pk, 0, h1, 'h1', m)
                    ps2, pk2 = pp.get()
                    kb.op('pe', lambda ps2=ps2, m=m: nc.tensor.matmul(ps2[0:64, 0:m], w2t[:, :], h1[:, 0:m], start=True, stop=True), reads=['w2t', 'h1'], writes=[pk2])
                    sin_of(ps2[0:64, 0:m], pk2, 1, h2, 'h2', m)
                    ps3, pk3 = pp.get()
                    col = (0 if which == 0 else 256) + j * 128
                    kb.op('pe', lambda ps3=ps3, m=m, col=col: nc.tensor.matmul(ps3[:, 0:m], w3t[:, col:col + 128], h2[:, 0:m], start=True, stop=True), reads=['w3t', 'h2'], writes=[pk3])
                    g0 = p0 if which == 1 else n - 1 + p0
                    if which == 0 and p0 == 0:
                        kb.op('dve', lambda: nc.vector.tensor_copy(sm[:, 38:39], G[:, n - 1:n]), reads=['G'], writes=['sm'])
                    kb.op('dve', lambda ps3=ps3, m=m, g0=g0: nc.vector.tensor_tensor(G[:, g0:g0 + m], ps3[:, 0:m], wt_[:, 0:m], ALU.mult), reads=[pk3, 'wt_', 'sm'], writes=['G'])
            kb.op('dve', lambda: nc.vector.tensor_tensor(G[:, n - 1:n], G[:, n - 1:n], sm[:, 38:39], ALU.add), reads=['G', 'sm'], writes=['G'])
            kb.op('dve', lambda: nc.vector.tensor_scalar_mul(G[:, n - 1:n], G[:, n - 1:n], 0.5), reads=['G'], writes=['G'])
            npc = 0
            for c0 in range(0, 2 * n - 1, NMAX if NMAX < 2048 else 2048):
                c1 = min(2 * n - 1, c0 + (NMAX if NMAX < 2048 else 2048))
                kb.op('act', lambda c0=c0, c1=c1: nc.scalar.activation(A1[:, 0:c1 - c0], G[:, c0:c1], AF.Abs), reads=['G'], writes=['A1'])
                kb.op('dve', lambda c0=c0, c1=c1, npc=npc: nc.vector.reduce_sum(sm[:, npc:npc + 1], A1[:, 0:c1 - c0], AX.X), reads=['A1'], writes=['sm'])
                npc += 1
            kb.op('dve', lambda npc=npc: nc.vector.reduce_sum(sm[:, 36:37], sm[:, 0:npc], AX.X), reads=['sm'], writes=['sm'])
            kb.op('dve', lambda: nc.vector.reciprocal(sm[:, 37:38], sm[:, 36:37]), reads=['sm'], writes=['sm'])
            kb.op('dve', lambda: nc.vector.tensor_scalar_mul(G[:, 0:2 * n - 1], G[:, 0:2 * n - 1], sm[:, 37:38]), reads=['G', 'sm'], writes=['G'])
            kb.op('dve', lambda: nc.vector.tensor_scalar_mul(A1[:, 0:n], Z[:, 0:n], G[:, n - 1:n]), reads=['Z', 'G', 'A1'], writes=['A1'])
            kb.op('pool', lambda: nc.gpsimd.memset(A2[:, 0:n], 0.0), writes=['A2'])
            for lag in range(1, n):
                kb.op('dve', lambda lag=lag: nc.vector.scalar_tensor_tensor(A1[:, 0:n - lag], Z[:, lag:n], G[:, n - 1 - lag:n - lag], A1[:, 0:n - lag], ALU.mult, ALU.add),
                      reads=['Z', 'G', 'A1'], writes=['A1'])
                kb.op('dve', lambda lag=lag: nc.vector.scalar_tensor_tensor(A2[:, lag:n], Z[:, 0:n - lag], G[:, n - 1 + lag:n + lag], A2[:, lag:n], ALU.mult, ALU.add),
                      reads=['Z', 'G', 'A2'], writes=['A2'])
            kb.op('dve', lambda: nc.vector.tensor_tensor(A1[:, 0:n], A1[:, 0:n], A2[:, 0:n], ALU.add), reads=['A1', 'A2'], writes=['A1'])
            kb.op('dve', lambda: nc.vector.scalar_tensor_tensor(A1[:, 0:n], Z[:, 0:n], hbt[:, j:j + 1], A1[:, 0:n], ALU.mult, ALU.add), reads=['Z', 'hbt', 'A1'], writes=['A1'])
            kb.dma('pool', G[:, 0:n], hp.ap()[j * 128:(j + 1) * 128, t0:t0 + n], reads=[], writes=['G'])
            sconv(A2, 'A2', G, 'G', j, n)
            kb.op('dve', lambda: nc.vector.tensor_tensor(A1[:, 0:n], A1[:, 0:n], A2[:, 0:n], ALU.mult), reads=['A1', 'A2'], writes=['A1'])
            kb.dma('pool', hyo.ap()[j * 128:(j + 1) * 128, t0:t0 + n], A1[:, 0:n], reads=['A1'])
    kb.finish('sp')
    kb.close()
    return nc


def hy_tables(segs, qd):
    T = sum(n for _, n in segs)
    zt = np.zeros((2, 33, T), np.float32)
    wt = np.zeros((2, 2, 128, T), np.float32)
    deltas = np.abs(np.linspace(math.log(1e-2) / 1.5, math.log(1e-2) / 0.3, 1024, dtype=np.float32))[qd * 256:(qd + 1) * 256]
    for (t0, n) in segs:
        t = np.linspace(0.0, 1.0, n, dtype=np.float32)[:, None]
        lag = np.arange(n, dtype=np.float32)[:, None]
        bands = np.linspace(1e-4, 15, 16, dtype=np.float32)[None, :]
        ang = (2.0 * np.float32(math.pi) * lag * bands / np.float32(n)).astype(np.float32)
        z = np.concatenate([t, np.cos(ang), -np.sin(ang)], axis=-1).astype(np.float32)
        win = np.exp(-t * deltas[None, :]).astype(np.float32)
        zt[0, :, t0:t0 + n] = z.T
        zt[1, :, t0:t0 + n] = z[::-1].T
        for j in range(2):
            wt[0, j, :, t0:t0 + n] = win[:, j * 128:(j + 1) * 128].T
            wt[1, j, :, t0:t0 + n] = win[::-1, j * 128:(j + 1) * 128].T
    return zt, wt
_HY_TAB = {}


def run_hyena(L, last, pTs, P):
    T = CTX + SEQ
    segs = [(0, SEQ)] if last else [(0, CTX), (CTX, SEQ)]
    nc = build_hy(segs, SEQ)
    ins = []
    w3 = P['hy_f_w3'][L]
    fv = np.ascontiguousarray(np.stack([P['hy_f_b1'][L], P['hy_f_b2'][L], P['hy_freq'][L], np.full(64, -math.pi, np.float32)], 1).astype(np.float32))
    for i in range(8):
        b, qd = i // 4, i % 4
        hp = pTs[i][R_HY:R_HY + 768]
        hp = np.ascontiguousarray(hp[:, CTX:] if last else hp)
        cw = np.zeros((128, 6, 4), np.float32)
        for k in range(3):
            for j in range(2):
                cs = slice(k * 1024 + qd * 256 + j * 128, k * 1024 + qd * 256 + (j + 1) * 128)
                cw[:, k * 2 + j, 0:3] = P['hy_conv_w'][L][:, cs].T
                cw[:, k * 2 + j, 3] = P['hy_conv_b'][L][cs]
        fw3 = np.ascontiguousarray(np.concatenate([w3[:, qd * 256:(qd + 1) * 256], w3[:, 1024 + qd * 256:1024 + (qd + 1) * 256]], 1))
        hb = np.ascontiguousarray(P['hy_bias'][L][qd * 256:(qd + 1) * 256].reshape(2, 128).T)
        key = (tuple(segs), qd)
        if key not in _HY_TAB:
            _HY_TAB[key] = hy_tables(segs, qd)
        zt, wt = _HY_TAB[key]
        ins.append({"hp": hp, "cw": cw, "fw1": P['hy_f_w1'][L], "fw2": P['hy_f_w2'][L], "fw3": fw3, "fv": fv, "hb": hb, "ztab": zt, "wtab": wt})
    res = run_bass_kernel_spmd(nc, ins, core_ids=list(range(8)))
    oc = np.zeros((2, T, 1024), np.float32)
    for i in range(8):
        b, qd = i // 4, i % 4
        hyo = res.results[i]["hyo"].T
        if last:
            oc[b, CTX:, qd * 256:(qd + 1) * 256] = hyo
        else:
            oc[b, :, qd * 256:(qd + 1) * 256] = hyo
    return oc
RW_NCH = 11


def build_rwp(n_ctx, n_lat):
    nc = bass.Bass("TRN2", target_bir_lowering=False)
    kb = KB(nc)
    T = n_ctx + n_lat
    NT = 512
    rp = nc.dram_tensor("rp", [1376, T], F32, kind="ExternalInput")
    tsc = nc.dram_tensor("tsc", [128, RW_NCH, 7], F32, kind="ExternalInput")
    lw = nc.dram_tensor("lw", [128, 5, 256], F32, kind="ExternalInput")
    rv = nc.dram_tensor("rv", [128, 2, 8], F32, kind="ExternalInput")
    bones_d = nc.dram_tensor("bones", [128, 128], F32, kind="ExternalInput")
    rwo = nc.dram_tensor("rwo", [11 * 256, T], F32, kind="ExternalOutput")
    mT = kb.dram("mT", [1376, T])
    pp = PsPool(kb, 8)
    P_ = kb.sb("P_", [128, T]); M_ = kb.sb("M_", [128, T])
    tst = kb.sb("tst", [128, RW_NCH, 7]); cf = kb.sb("cf", [128, RW_NCH, 7])
    lwt = kb.sb("lwt", [128, 5, 256]); rvt = kb.sb("rvt", [128, 2, 8]); bones = kb.sb("bones", [128, 128])
    omka = kb.sb("omka", [128, 2])
    kb.dma('pool', tst[:], tsc.ap(), writes=['tst'])
    kb.dma('pool', lwt[:], lw.ap(), writes=['lwt'])
    kb.dma('pool', rvt[:], rv.ap(), writes=['rvt'])
    kb.dma('pool', bones[:], bones_d.ap(), writes=['bones'])
    V = nc.vector
    kb.op('dve', lambda: V.tensor_scalar(cf[:, :, 0:1], tst[:, :, 0:1], -1.0, 1.0, ALU.mult, ALU.add), reads=['tst'], writes=['cf'])
    kb.op('dve', lambda: V.tensor_tensor(cf[:, :, 1:7], tst[:, :, 1:7], tst[:, :, 0:1].broadcast_to([128, RW_NCH, 6]), ALU.mult), reads=['tst'], writes=['cf'])
    kb.op('dve', lambda: V.tensor_scalar(omka[:], rvt[:, :, 5], -1.0, 1.0, ALU.mult, ALU.add), reads=['rvt'], writes=['omka'])
    for c in range(RW_NCH):
        nr = min(128, 1376 - c * 128)
        kb.dma('sp', P_[0:nr, :], rp.ap()[c * 128:c * 128 + nr, :], writes=['P_'])
        kb.op('dve', lambda c=c, nr=nr: V.tensor_scalar_mul(M_[0:nr, :], P_[0:nr, :], cf[0:nr, c, 0:1]), reads=['P_', 'cf'], writes=['M_'])
        Pl = P_[0:nr, n_ctx:T].rearrange("p (r w) -> p r w", w=64)
        Ml = M_[0:nr, n_ctx:T].rearrange("p (r w) -> p r w", w=64)
        R = n_lat // 64
        specs = [(Ml[:, :, 1:64], Pl[:, :, 0:63], 1), (Ml[:, :, 0:63], Pl[:, :, 1:64], 2),
                 (Ml[:, 1:R, :], Pl[:, 0:R - 1, :], 3), (Ml[:, 0:R - 1, :], Pl[:, 1:R, :], 4)]
        if n_ctx:
            specs += [(M_[0:nr, 1:n_ctx], P_[0:nr, 0:n_ctx - 1], 5), (M_[0:nr, 0:n_ctx - 1], P_[0:nr, 1:n_ctx], 6)]
        for (o_, i_, ci) in specs:
            kb.op('dve', lambda o_=o_, i_=i_, ci=ci, c=c, nr=nr: V.scalar_tensor_tensor(o_, i_, cf[0:nr, c, ci:ci + 1], o_, ALU.mult, ALU.add),
                  reads=['P_', 'cf', 'M_'], writes=['M_'])
        kb.dma('pool', mT.ap()[c * 128:c * 128 + nr, :], M_[0:nr, :], reads=['M_'], writes=['mT'])
    mt = kb.sb("mt", [128, RW_NCH, NT])
    names = ['tw0', 'tw1', 'sg', 'e1', 'wd', 'a0', 'a1', 'gout', 'kraw', 'sq', 'rn', 'kk', 'akn', 't1', 'kd0', 'kd1', 'bp0', 'bp1', 'rk0', 'rk1', 'bon']
    tl = {nm: kb.sb(nm, [128, NT]) for nm in names}

    def store(slot, j, t0, n, tile_ap, key):
        kb.dma('pool', rwo.ap()[slot * 256 + j * 128:slot * 256 + (j + 1) * 128, t0:t0 + n], tile_ap, reads=[key])

    for (t0, n, _) in seg_tiles(n_ctx, n_lat, NT):
        kb.dma('sp', mt[:, 0:10, 0:n], mT.ap()[0:1280, t0:t0 + n].rearrange("(c p) t -> p c t", p=128), reads=['mT'], writes=['mt'])
        kb.dma('sp', mt[0:96, 10, 0:n], mT.ap()[1280:1376, t0:t0 + n], reads=['mT'], writes=['mt'])
        for d in range(2):
            kb.op('act', lambda d=d: nc.scalar.activation(tl[f'tw{d}'][:, 0:n], mt[:, 6 + d, 0:n], AF.Tanh), reads=['mt'], writes=[f'tw{d}'])
        kb.op('act', lambda: nc.scalar.activation(tl['sg'][0:96, 0:n], mt[0:96, 10, 0:n], AF.Sigmoid), reads=['mt'], writes=['sg'])
        for j in range(2):
            cs = slice(j * 128, (j + 1) * 128)
            for d in range(2):
                ps, pk = pp.get()
                kb.op('pe', lambda ps=ps, d=d: nc.tensor.matmul(ps[:, 0:n], lwt[:, d, cs], tl[f'tw{d}'][:, 0:n], start=True, stop=True), reads=['lwt', f'tw{d}'], writes=[pk])
                kb.op('act', lambda ps=ps, d=d: nc.scalar.activation(tl['e1'][:, 0:n], ps[:, 0:n], AF.Sigmoid, bias=rvt[:, j, d:d + 1]), reads=[pk, 'rvt'], writes=['e1'])
                kb.op('act', lambda: nc.scalar.activation(tl['wd'][:, 0:n], tl['e1'][:, 0:n], AF.Exp, scale=-0.6065306597126334), reads=['e1'], writes=['wd'])
                store(3 + d, j, t0, n, tl['wd'][:, 0:n], 'wd')
            for d in range(2):
                ps, pk = pp.get()
                kb.op('pe', lambda ps=ps, d=d: nc.tensor.matmul(ps[:, 0:n], lwt[:, 2 + d, cs], mt[:, 8 + d, 0:n], start=True, stop=True), reads=['lwt', 'mt'], writes=[pk])
                kb.op('act', lambda ps=ps, d=d: nc.scalar.activation(tl[f'a{d}'][:, 0:n], ps[:, 0:n], AF.Sigmoid, bias=rvt[:, j, 2 + d:3 + d]), reads=[pk, 'rvt'], writes=[f'a{d}'])
            ps, pk = pp.get()
            kb.op('pe', lambda ps=ps: nc.tensor.matmul(ps[:, 0:n], lwt[0:96, 4, cs], tl['sg'][0:96, 0:n], start=True, stop=True), reads=['lwt', 'sg'], writes=[pk])
            kb.op('act', lambda ps=ps: nc.scalar.copy(tl['gout'][:, 0:n], ps[:, 0:n]), reads=[pk], writes=['gout'])
            store(9, j, t0, n, tl['gout'][:, 0:n], 'gout')
            kb.op('dve', lambda: V.tensor_scalar_mul(tl['kraw'][:, 0:n], mt[:, 2 + j, 0:n], rvt[:, j, 4:5]), reads=['mt', 'rvt'], writes=['kraw'])
            kb.op('dve', lambda: V.tensor_tensor(tl['sq'][:, 0:n], tl['kraw'][:, 0:n], tl['kraw'][:, 0:n], ALU.mult), reads=['kraw'], writes=['sq'])
            ps, pk = pp.get()
            kb.op('pe', lambda ps=ps: nc.tensor.matmul(ps[:, 0:n], bones[:], tl['sq'][:, 0:n], start=True, stop=True), reads=['bones', 'sq'], writes=[pk])
            kb.op('dve', lambda ps=ps: V.tensor_scalar_max(tl['rn'][:, 0:n], ps[:, 0:n], 1e-24), reads=[pk], writes=['rn'])
            kb.op('act', lambda: nc.scalar.sqrt(tl['rn'][:, 0:n], tl['rn'][:, 0:n]), reads=['rn'], writes=['rn'])
            kb.op('dve', lambda: V.reciprocal(tl['rn'][:, 0:n], tl['rn'][:, 0:n]), reads=['rn'], writes=['rn'])
            kb.op('dve', lambda: V.tensor_tensor(tl['kk'][:, 0:n], tl['kraw'][:, 0:n], tl['rn'][:, 0:n], ALU.mult), reads=['kraw', 'rn'], writes=['kk'])
            kb.op('dve', lambda: V.tensor_scalar_mul(tl['akn'][:, 0:n], tl['kk'][:, 0:n], -1.0), reads=['kk'], writes=['akn'])
            store(2, j, t0, n, tl['akn'][:, 0:n], 'akn')
            for d in range(2):
                kb.op('dve', lambda d=d: V.tensor_scalar(tl['t1'][:, 0:n], tl[f'a{d}'][:, 0:n], rvt[:, j, 5:6], omka[:, j:j + 1], ALU.mult, ALU.add),
                      reads=[f'a{d}', 'rvt', 'omka'], writes=['t1'])
                kb.op('dve', lambda d=d: V.tensor_tensor(tl[f'kd{d}'][:, 0:n], mt[:, 2 + j, 0:n], tl['t1'][:, 0:n], ALU.mult), reads=['mt', 't1'], writes=[f'kd{d}'])
                store(5 + d, j, t0, n, tl[f'kd{d}'][:, 0:n], f'kd{d}')
                kb.op('dve', lambda d=d: V.tensor_tensor(tl[f'bp{d}'][:, 0:n], tl['kk'][:, 0:n], tl[f'a{d}'][:, 0:n], ALU.mult), reads=['kk', f'a{d}'], writes=[f'bp{d}'])
                store(7 + d, j, t0, n, tl[f'bp{d}'][:, 0:n], f'bp{d}')
                kb.op('dve', lambda d=d: V.tensor_tensor(tl[f'rk{d}'][:, 0:n], mt[:, j, 0:n], tl[f'kd{d}'][:, 0:n], ALU.mult), reads=['mt', f'kd{d}'], writes=[f'rk{d}'])
            kb.op('dve', lambda: V.tensor_tensor(tl['rk0'][:, 0:n], tl['rk0'][:, 0:n], tl['rk1'][:, 0:n], ALU.add), reads=['rk0', 'rk1'], writes=['rk0'])
            kb.op('dve', lambda: V.tensor_scalar_mul(tl['rk0'][:, 0:n], tl['rk0'][:, 0:n], rvt[:, j, 6:7]), reads=['rk0', 'rvt'], writes=['rk0'])
            ps, pk = pp.get()
            kb.op('pe', lambda ps=ps: nc.tensor.matmul(ps[:, 0:n], bones[:], tl['rk0'][:, 0:n], start=True, stop=True), reads=['bones', 'rk0'], writes=[pk])
            kb.op('dve', lambda ps=ps: V.tensor_tensor(tl['bon'][:, 0:n], ps[:, 0:n], mt[:, 4 + j, 0:n], ALU.mult), reads=[pk, 'mt'], writes=['bon'])
            store(10, j, t0, n, tl['bon'][:, 0:n], 'bon')
            store(0, j, t0, n, mt[:, j, 0:n], 'mt')
            store(1, j, t0, n, mt[:, 4 + j, 0:n], 'mt')
    kb.finish('sp')
    kb.close()
    return nc


def build_rws(T, TB=8):
    nc = bass.Bass("TRN2", target_bir_lowering=False)
    kb = KB(nc)
    qb = nc.dram_tensor("qb", [5, T, 2, 256], F32, kind="ExternalInput")
    vv = nc.dram_tensor("vv", [128, T, 4], F32, kind="ExternalInput")
    yo = nc.dram_tensor("yo", [128, T, 4], F32, kind="ExternalOutput")
    qt = [kb.sb(f"qt{i}", [128, 5, TB, 256]) for i in range(2)]
    kv = [kb.sb(f"kv{i}", [128, TB, 256]) for i in range(2)]
    vt = [kb.sb(f"vt{i}", [128, TB, 4]) for i in range(2)]
    yt = [kb.sb(f"yt{i}", [128, TB, 4]) for i in range(2)]
    S = [kb.sb(f"S{i}", [128, 256]) for i in range(2)]
    tmp = kb.sb("tmp", [128, 256]); sa = kb.sb("sa", [128, 4])
    yb_ = [kb.sb(f"ybuf{i}", [128, TB, 256]) for i in range(2)]
    V = nc.vector; G = nc.gpsimd
    kb.op('dve', lambda: V.memset(S[0][:], 0.0), writes=['S0'])
    par = 0
    for bi, t0 in enumerate(range(0, T, TB)):
        b = bi % 2
        for qi in range(5):
            for h2 in range(2):
                src = bass.AP(tensor=qb, offset=(qi * T + t0) * 512 + h2 * 256, ap=[[0, 64], [512, TB], [1, 256]])
                kb.dma('sp' if (qi * 2 + h2) % 2 == 0 else 'act', qt[b][h2 * 64:(h2 + 1) * 64, qi, :, :], src, writes=[f'qt{b}'])
        kb.dma('sp', vt[b][:], vv.ap()[:, t0:t0 + TB, :], writes=[f'vt{b}'])
        kvv = kv[b][:].rearrange("p t (g k) -> p (t g) k", k=64)
        kin = qt[b][:, 3, :, :].rearrange("p t (g k) -> p (t g) k", k=64)
        vin = vt[b][:].rearrange("p t g -> p (t g)").unsqueeze(2).broadcast_to([128, TB * 4, 64])
        kb.op('pool', lambda kvv=kvv, kin=kin, vin=vin: G.tensor_tensor(kvv, kin, vin, ALU.mult), reads=[f'qt{b}', f'vt{b}'], writes=[f'kv{b}'])
        for t in range(TB):
            Sa, Sb = S[par], S[1 - par]
            ka, kb_ = f'S{par}', f'S{1 - par}'
            kb.op('dve', lambda Sa=Sa, t=t, b=b: V.tensor_tensor(tmp[:], Sa[:], qt[b][:, 1, t, :], ALU.mult), reads=[ka, f'qt{b}'], writes=['tmp'])
            kb.op('dve', lambda: V.tensor_reduce(sa[:], tmp[:].rearrange("p (g k) -> p g k", k=64), AX.X, ALU.add), reads=['tmp'], writes=['sa'])
            kb.op('dve', lambda Sa=Sa, Sb=Sb, t=t, b=b: V.tensor_tensor(Sb[:], Sa[:], qt[b][:, 0, t, :], ALU.mult), reads=[ka, f'qt{b}'], writes=[kb_])
            kb.op('dve', lambda t=t, b=b: V.tensor_tensor(tmp[:].rearrange("p (g k) -> p g k", k=64), qt[b][:, 2, t, :].rearrange("p (g k) -> p g k", k=64),
                                                      sa[:].unsqueeze(2).broadcast_to([128, 4, 64]), ALU.mult), reads=['sa', f'qt{b}'], writes=['tmp'])
            kb.op('dve', lambda Sb=Sb: V.tensor_tensor(Sb[:], Sb[:], tmp[:], ALU.add), reads=['tmp', kb_], writes=[kb_])
            kb.op('dve', lambda Sb=Sb, t=t, b=b: V.tensor_tensor(Sb[:], Sb[:], kv[b][:, t, :], ALU.add), reads=[f'kv{b}', kb_], writes=[kb_])
            kb.op('pool', lambda Sb=Sb, t=t, b=b: G.tensor_tensor(yb_[b][:, t, :], Sb[:], qt[b][:, 4, t, :], ALU.mult), reads=[kb_, f'qt{b}'], writes=[f'ybuf{b}'])
            par = 1 - par
        kb.op('dve', lambda b=b: V.tensor_reduce(yt[b][:].rearrange("p t g -> p (t g)"), yb_[b][:].rearrange("p t (g k) -> p (t g) k", k=64), AX.X, ALU.add),
              reads=[f'ybuf{b}'], writes=[f'yt{b}'])
        kb.dma('pool', yo.ap()[:, t0:t0 + TB, :], yt[b][:], reads=[f'yt{b}'])
    kb.finish('sp')
    kb.close()
    return nc
def _idx_dirs(n_ctx, n_lat):
    nat = np.arange(n_ctx + n_lat)
    rev = np.concatenate([np.arange(n_ctx)[::-1], n_ctx + np.arange(n_lat)[::-1]])
    return [nat, rev]


def rw_prep_inputs(pT, qd, L, P):
    rp = np.ascontiguousarray(np.concatenate([pT[0:1280], pT[R_GD:R_GD + 96]], 0))
    full = np.concatenate([np.arange(qd * 256, (qd + 1) * 256), 1024 + np.arange(qd * 256, (qd + 1) * 256),
                           2048 + np.arange(qd * 256, (qd + 1) * 256), np.arange(3072, 3680)])
    mu = P['rw_mu'][L][full]
    grp = full // 920
    tsc = np.zeros((128, RW_NCH, 7), np.float32)
    vals = np.stack([mu, grp == 0, grp == 1, grp == 2, grp == 3, full < 1840, full >= 1840], 1).astype(np.float32)
    pad = np.zeros((RW_NCH * 128, 7), np.float32); pad[:1376] = vals
    tsc[:] = pad.reshape(RW_NCH, 128, 7).transpose(1, 0, 2)
    cs = slice(qd * 256, (qd + 1) * 256)
    lw = np.zeros((128, 5, 256), np.float32)
    lw[:, 0] = P['rw_w_up'][L][0][:, cs]; lw[:, 1] = P['rw_w_up'][L][1][:, cs]
    lw[:, 2] = P['rw_a_up'][L][0][:, cs]; lw[:, 3] = P['rw_a_up'][L][1][:, cs]
    lw[0:96, 4] = P['rw_g_up'][L][:, cs]
    rk = P['rw_r_k'][L].reshape(-1)
    vecs = [P['rw_w0'][L][0], P['rw_w0'][L][1], P['rw_a0'][L][0], P['rw_a0'][L][1], P['rw_k_k'][L], P['rw_k_a'][L], rk, rk]
    rv = np.ascontiguousarray(np.stack([v[cs].reshape(2, 128).T for v in vecs], axis=2).astype(np.float32))
    bones = np.kron(np.eye(2, dtype=np.float32), np.ones((64, 64), np.float32))
    return {"rp": rp, "tsc": tsc, "lw": lw, "rv": rv, "bones": bones}


def rw_scan_inputs(rwo, n_ctx, n_lat):
    T = n_ctx + n_lat
    idx = _idx_dirs(n_ctx, n_lat)
    A = lambda slot: rwo[slot * 256:(slot + 1) * 256]
    qb = np.empty((5, T, 2, 256), np.float32)
    vv = np.empty((128, T, 4), np.float32)
    for d in range(2):
        srcs = [A(3 + d), A(2), A(7 + d), A(5 + d), A(0)]
        for j in range(2):
            g = d * 2 + j
            for h2 in range(2):
                rows = slice((2 * j + h2) * 64, (2 * j + h2 + 1) * 64)
                for qi in range(5):
                    qb[qi, :, h2, g * 64:(g + 1) * 64] = srcs[qi][rows][:, idx[d]].T
                vv[h2 * 64:(h2 + 1) * 64, :, g] = A(1)[rows][:, idx[d]]
    return {"qb": qb, "vv": vv}


def rw_collect(yo, n_ctx, n_lat):
    T = n_ctx + n_lat
    idx = _idx_dirs(n_ctx, n_lat)
    ys = [np.empty((T, 256), np.float32) for _ in range(2)]
    for d in range(2):
        for j in range(2):
            for h2 in range(2):
                hl = 2 * j + h2
                ys[d][idx[d], hl * 64:(hl + 1) * 64] = yo[h2 * 64:(h2 + 1) * 64, :, d * 2 + j].T
    return ys


def run_rwkv(L, last, pTs, P):
    T = CTX + SEQ
    nc = build_rwp(CTX, SEQ)
    ins = [rw_prep_inputs(pTs[i], i % 4, L, P) for i in range(8)]
    res = run_bass_kernel_spmd(nc, ins, core_ids=list(range(8)))
    rwos = [res.results[i]["rwo"] for i in range(8)]
    del ins
    nc = build_rws(T)
    ins = [rw_scan_inputs(rwos[i], CTX, SEQ) for i in range(8)]
    res = run_bass_kernel_spmd(nc, ins, core_ids=list(range(8)))
    ryf = np.empty((2, T, 1024), np.float32); ryb = np.empty_like(ryf); rgg = np.empty_like(ryf); rbn = np.empty_like(ryf)
    for i in range(8):
        b, qd = i // 4, i % 4
        yf, yb = rw_collect(res.results[i]["yo"], CTX, SEQ)
        cs = slice(qd * 256, (qd + 1) * 256)
        ryf[b, :, cs] = yf; ryb[b, :, cs] = yb
        rgg[b, :, cs] = rwos[i][9 * 256:10 * 256].T
        rbn[b, :, cs] = rwos[i][10 * 256:11 * 256].T
    return ryf, ryb, rgg, rbn
def _w1_cols(qd):
    def rw(base):
        return list(range(base + qd * 256, base + (qd + 1) * 256))
    cols = rw(0) + rw(1024) + rw(2048) + list(range(3072, 3584))
    for k in range(5):
        cols += rw(3680 + k * 1024)
    for k in range(3):
        cols += rw(8800 + k * 1024)
    cols += list(range(3584, 3680))
    return np.array(cols)


def _rev_seg(a, n_ctx):
    return np.concatenate([a[:, :n_ctx][:, ::-1], a[:, n_ctx:][:, ::-1]], axis=1)


def run_stage1(L, last, xl, xc, mods, P):
    T = CTX + SEQ
    ones = np.ones((128, 128), np.float32)
    nc = build_stage1a(CTX, SEQ, L)
    ins = []
    xTs = [np.ascontiguousarray(np.concatenate([xc[b], xl[b]], 0).T) for b in range(2)]
    for i in range(8):
        b, qd = i // 4, i % 4
        vec = [mods[L][b][0:D], mods[L][b][D:2 * D], mods[L][2][0:D], mods[L][2][D:2 * D], P['norm1_g'][L]]
        vecs = np.ascontiguousarray(np.stack([_fm(v) for v in vec], axis=1))
        ins.append({"xT": xTs[b], "w1": np.ascontiguousarray(P['w_in'][L][:, _w1_cols(qd)]), "vecs": vecs, "ones": ones})
    res = run_bass_kernel_spmd(nc, ins, core_ids=list(range(8)))
    pTs = [res.results[i]["pT"] for i in range(8)]
    del ins, xTs
    nc = build_hg(T, L)
    ins = []
    lbs = P['hg_lower_bounds']
    for i in range(8):
        b, qd = i // 4, i % 4
        pT = pTs[i]
        hq = np.empty((2, 2, 128, T), np.float32); hz = np.empty_like(hq); hi = np.empty_like(hq)
        hgl = np.empty((128, 2, 2, 2), np.float32)
        for hh in range(2):
            r = lambda k: pT[R_HG + k * 256 + hh * 128: R_HG + k * 256 + (hh + 1) * 128]
            hq[0, hh] = r(0); hq[1, hh] = _rev_seg(r(0), CTX)
            hz[0, hh] = r(1); hz[1, hh] = _rev_seg(r(2), CTX)
            hi[0, hh] = r(3); hi[1, hh] = _rev_seg(r(3), CTX)
            hd = 2 * qd + hh
            for l in range(2):
                for d in range(2):
                    hgl[:, l, d, hh] = lbs[l, d, hd * 128:(hd + 1) * 128]
        ins.append({"hq": hq, "hz": hz, "hi": hi, "hgl": hgl, "ones": ones})
    res = run_bass_kernel_spmd(nc, ins, core_ids=list(range(8)))
    obf = np.empty((2, T, 1024), np.float32); obb = np.empty_like(obf); ogg = np.empty_like(obf)
    for i in range(8):
        b, qd = i // 4, i % 4
        ho = res.results[i]["ho"]
        for hh in range(2):
            hd = 2 * qd + hh
            obf[b, :, hd * 128:(hd + 1) * 128] = ho[0, hh].T
            obb[b, :, hd * 128:(hd + 1) * 128] = _rev_seg(ho[1, hh], CTX).T
            ogg[b, :, hd * 128:(hd + 1) * 128] = pTs[i][R_HG + 4 * 256 + hh * 128: R_HG + 4 * 256 + (hh + 1) * 128].T
    oc = run_hyena(L, last, pTs, P)
    rwx = run_rwkv(L, last, pTs, P)
    o_l = np.zeros((2, SEQ, 3072), np.float32)
    o_l[:, :, 2048:] = oc[:, CTX:]
    o_c = None
    if not last:
        o_c = np.zeros((2, CTX, 3072), np.float32)
        o_c[:, :, 2048:] = oc[:, :CTX]
    return o_l, o_c, (obf, obb, ogg), rwx
SEQ = 8192
CTX = 256
HY_END = 11872
_NC_CACHE = {}


def _fm(v):
    return np.ascontiguousarray(np.asarray(v, np.float32).reshape(-1, 128).T)


def _get_nc(key, fn):
    return fn()


def run_stage0(ada_w, ada_b, c, c_ctx):
    nc = build_stage0()
    ct = np.stack([_fm(c[0]), _fm(c[1]), _fm(c_ctx)], axis=2)
    ins = []
    for i in range(8):
        L, j = i // 4, i % 4
        ins.append({"aw": np.ascontiguousarray(ada_w[L][:, j * 6144:(j + 1) * 6144]),
                    "ab": np.ascontiguousarray(ada_b[L][j * 6144:(j + 1) * 6144].reshape(48, 128).T),
                    "ct": ct})
    res = run_bass_kernel_spmd(nc, ins, core_ids=list(range(8)))
    mods = np.zeros((2, 3, 24576), np.float32)
    for i in range(8):
        L, j = i // 4, i % 4
        mo = res.results[i]["mo"]
        mods[L][:, j * 6144:(j + 1) * 6144] = mo.transpose(2, 1, 0).reshape(3, 6144)
    return mods


def run_stage2(L, last, xl, xc, o_l, o_c, hgx, rwx, mods, P):
    n_lat = SEQ // 4
    n_ctx = 0 if last else CTX // 4
    nc = build_stage2(n_ctx, n_lat, last)
    wgi = np.ascontiguousarray(P['w_in'][L][:, HY_END:])
    wbr = np.ascontiguousarray(np.concatenate([P['w_branch_a'][L], P['w_branch_b'][L], P['w_branch_c'][L]], 0))
    ones = np.ones((128, 128), np.float32)
    bones = np.kron(np.eye(2, dtype=np.float32), np.ones((64, 64), np.float32))
    rln = np.ascontiguousarray(np.stack([P['rw_ln_g'][L].reshape(8, 128).T, P['rw_ln_b'][L].reshape(8, 128).T], 2).astype(np.float32))
    ins = []
    for i in range(8):
        b, q = i // 4, i % 4
        vec = [mods[L][b][k * D:(k + 1) * D] for k in range(6)] + [mods[L][2][k * D:(k + 1) * D] for k in range(6)]
        vec += [P['norm1_g'][L], P['norm2_g'][L], P['final_norm_g']]
        vecs = np.ascontiguousarray(np.stack([_fm(v) for v in vec], axis=1))
        xs = xl[b, q * n_lat:(q + 1) * n_lat]
        os_ = o_l[b, q * n_lat:(q + 1) * n_lat]
        hx = [a[b, CTX + q * n_lat:CTX + (q + 1) * n_lat] for a in hgx]
        rx = [a[b, CTX + q * n_lat:CTX + (q + 1) * n_lat] for a in rwx]
        if n_ctx:
            hx = [np.concatenate([a[b, q * n_ctx:(q + 1) * n_ctx], h_], 0) for a, h_ in zip(hgx, hx)]
            rx = [np.concatenate([a[b, q * n_ctx:(q + 1) * n_ctx], h_], 0) for a, h_ in zip(rwx, rx)]
        if n_ctx:
            xs = np.concatenate([xc[b, q * n_ctx:(q + 1) * n_ctx], xs], 0)
            os_ = np.concatenate([o_c[b, q * n_ctx:(q + 1) * n_ctx], os_], 0)
        ins.append({"xT": np.ascontiguousarray(xs.T), "oT": np.ascontiguousarray(os_.T), "wgi": wgi, "wbr": wbr,
                    "wo": P['w_out'][L], "fg": P['ffn_w_gate'][L], "fu": P['ffn_w_up'][L], "fd": P['ffn_w_down'][L],
                    "vecs": vecs, "ones": ones,
                    "obf": np.ascontiguousarray(hx[0].T), "obb": np.ascontiguousarray(hx[1].T), "ogg": np.ascontiguousarray(hx[2].T),
                    "hgn": np.ascontiguousarray(P['hg_norm_g'][L].reshape(128, 1)),
                    "ryf": np.ascontiguousarray(rx[0].T), "ryb": np.ascontiguousarray(rx[1].T), "rgg": np.ascontiguousarray(rx[2].T),
                    "rbn": np.ascontiguousarray(rx[3].T), "rln": rln, "bones": bones})
    res = run_bass_kernel_spmd(nc, ins, core_ids=list(range(8)))
    xl2 = np.empty_like(xl)
    xc2 = np.empty_like(xc) if n_ctx else None
    for i in range(8):
        b, q = i // 4, i % 4
        xo = res.results[i]["xo"].T
        if n_ctx:
            xc2[b, q * n_ctx:(q + 1) * n_ctx] = xo[:n_ctx]
        xl2[b, q * n_lat:(q + 1) * n_lat] = xo[n_ctx:]
    return xl2, xc2


def kernel(**inputs):
    P = {k: np.asarray(v) for k, v in inputs.items()}
    xl = np.ascontiguousarray(P['x'], dtype=np.float32)
    xc = np.ascontiguousarray(P['ctx'], dtype=np.float32)
    mods = run_stage0(P['ada_w'], P['ada_b'], P['c'], P['c_ctx'])
    for L in range(2):
        last = (L == 1)
        o_l, o_c, hgx, rwx = run_stage1(L, last, xl, xc, mods, P)
        xl, xc = run_stage2(L, last, xl, xc, o_l, o_c, hgx, rwx, mods, P)
    return xl.astype(np.float32)
```
